# Optimizing a Trainium2 kernel written in Bass

```python
import jax, jax.numpy as jnp
from jax import lax
import numpy as np

D_MODEL = 1024
BATCH = 4
SEQ = 4096
DEPTH = 2

HEAD_DIM = 64
N_HEADS = D_MODEL // HEAD_DIM
N_SB_HEADS = N_HEADS // 2
N_CA_HEADS = N_HEADS - N_SB_HEADS
D_SB = N_SB_HEADS * HEAD_DIM
D_CA = N_CA_HEADS * HEAD_DIM
D_IN = 3 * D_SB + 3 * D_CA
D_FF = 4 * D_MODEL
CHUNK = 64
LEFT_CHUNKS = 8
BAND = (LEFT_CHUNKS + 1) * CHUNK
REL_CLIP = 128
N_REL = 2 * REL_CLIP + 1
Q_BLOCK = 128
EPS = 1e-6
NEG_INF = -1e30

kernel_name = "hybrid_stickbreak_chunkrel_adaln_encoder"


def rmsnorm(x, g):
    xf = x.astype(jnp.float32)
    y = xf * lax.rsqrt(jnp.mean(xf * xf, axis=-1, keepdims=True) + EPS)
    return (y * g.astype(jnp.float32)).astype(x.dtype)


def stick_breaking_attention(q, k, v):
    B, S, H, d = q.shape
    scale = d ** -0.5
    outs = []
    for start in range(0, S, Q_BLOCK):
        end = start + Q_BLOCK
        qb = q[:, start:end]
        kb = k[:, :end]
        vb = v[:, :end]
        z = jnp.einsum('bqhd,bkhd->bhqk', qb, kb).astype(jnp.float32) * scale
        t_idx = start + jnp.arange(Q_BLOCK)[:, None]
        s_idx = jnp.arange(end)[None, :]
        strict = s_idx < t_idx
        log_beta = jax.nn.log_sigmoid(z)
        log_1m_beta = jnp.where(strict, jax.nn.log_sigmoid(-z), 0.0)
        suffix = lax.cumsum(log_1m_beta, axis=3, reverse=True) - log_1m_beta
        w = jnp.where(strict, jnp.exp(log_beta + suffix), 0.0)
        outs.append(jnp.einsum('bhqk,bkhd->bqhd', w.astype(v.dtype), vb))
    return jnp.concatenate(outs, axis=1)


def chunked_relpos_attention(q, k, v, rel_bias):
    B, S, H, d = q.shape
    nc = S // CHUNK
    pad = LEFT_CHUNKS * CHUNK
    kp = jnp.pad(k, ((0, 0), (pad, 0), (0, 0), (0, 0))).reshape(B, nc + LEFT_CHUNKS, CHUNK, H, d)
    vp = jnp.pad(v, ((0, 0), (pad, 0), (0, 0), (0, 0))).reshape(B, nc + LEFT_CHUNKS, CHUNK, H, d)
    k_band = jnp.concatenate([kp[:, i:i + nc] for i in range(LEFT_CHUNKS + 1)], axis=2)
    v_band = jnp.concatenate([vp[:, i:i + nc] for i in range(LEFT_CHUNKS + 1)], axis=2)
    qc = q.reshape(B, nc, CHUNK, H, d)
    s = jnp.einsum('bnqhd,bnkhd->bnhqk', qc, k_band).astype(jnp.float32) * (d ** -0.5)
    qi = jnp.arange(CHUNK)[:, None]
    kj = jnp.arange(BAND)[None, :]
    rel = qi + pad - kj
    rel_idx = jnp.clip(rel, -REL_CLIP, REL_CLIP) + REL_CLIP
    bias = rel_bias[:, rel_idx].astype(jnp.float32)
    s = s + bias[None, None]
    key_pos = jnp.arange(nc)[:, None] * CHUNK + kj - pad
    valid = key_pos >= 0
    s = jnp.where(valid[None, :, None, None, :], s, NEG_INF)
    p = jax.nn.softmax(s, axis=-1)
    out = jnp.einsum('bnhqk,bnkhd->bnqhd', p.astype(v.dtype), v_band)
    return out.reshape(B, S, H, d)


def setup_inputs(seed: int = 0) -> dict:
    key = jax.random.key(seed)
    ks = jax.random.split(key, 16)
    f32 = jnp.float32
    x = jax.random.normal(ks[0], (BATCH, SEQ, D_MODEL), f32)
    c = jax.random.normal(ks[1], (BATCH, D_MODEL), f32)
    g_norm1 = 1.0 + 0.01 * jax.random.normal(ks[2], (DEPTH, D_MODEL), f32)
    w_in = jax.random.normal(ks[3], (DEPTH, D_MODEL, D_IN), f32) * D_MODEL ** -0.5
    g_q = 1.0 + 0.01 * jax.random.normal(ks[4], (DEPTH, HEAD_DIM), f32)
    g_k = 1.0 + 0.01 * jax.random.normal(ks[5], (DEPTH, HEAD_DIM), f32)
    rel_bias = 0.1 * jax.random.normal(ks[6], (DEPTH, N_CA_HEADS, N_REL), f32)
    w_o = jax.random.normal(ks[7], (DEPTH, D_MODEL, D_MODEL), f32) * D_MODEL ** -0.5
    g_norm2 = 1.0 + 0.01 * jax.random.normal(ks[8], (DEPTH, D_MODEL), f32)
    w1 = jax.random.normal(ks[9], (DEPTH, D_MODEL, D_FF), f32) * D_MODEL ** -0.5
    w2 = jax.random.normal(ks[10], (DEPTH, D_FF, D_MODEL), f32) * D_FF ** -0.5
    w_ada = jax.random.normal(ks[11], (DEPTH, D_MODEL, 6 * D_MODEL), f32) * (0.5 * D_MODEL ** -0.5)
    b_ada = 0.01 * jax.random.normal(ks[12], (DEPTH, 6 * D_MODEL), f32)
    return {"x": x, "c": c, "g_norm1": g_norm1, "w_in": w_in, "g_q": g_q, "g_k": g_k,
            "rel_bias": rel_bias, "w_o": w_o, "g_norm2": g_norm2, "w1": w1, "w2": w2,
            "w_ada": w_ada, "b_ada": b_ada}


def reference(x, c, g_norm1, w_in, g_q, g_k, rel_bias, w_o, g_norm2, w1, w2, w_ada, b_ada):
    B, S, D = x.shape
    split_pts = [D_SB, 2 * D_SB, 3 * D_SB, 3 * D_SB + D_CA, 3 * D_SB + 2 * D_CA]
    c_act = jax.nn.silu(c)
    for l in range(DEPTH):
        mod = c_act @ w_ada[l] + b_ada[l]
        sh1, sc1, gt1, sh2, sc2, gt2 = [m[:, None, :] for m in jnp.split(mod, 6, axis=-1)]
        h = rmsnorm(x, g_norm1[l]) * (1.0 + sc1) + sh1
        proj = h @ w_in[l]
        q_sb, k_sb, v_sb, q_ca, k_ca, v_ca = jnp.split(proj, split_pts, axis=-1)
        hs = lambda t, n: t.reshape(B, S, n, HEAD_DIM)
        o_sb = stick_breaking_attention(hs(q_sb, N_SB_HEADS), hs(k_sb, N_SB_HEADS), hs(v_sb, N_SB_HEADS))
        q_ca = rmsnorm(hs(q_ca, N_CA_HEADS), g_q[l])
        k_ca = rmsnorm(hs(k_ca, N_CA_HEADS), g_k[l])
        o_ca = chunked_relpos_attention(q_ca, k_ca, hs(v_ca, N_CA_HEADS), rel_bias[l])
        mixed = jnp.concatenate([o_sb.reshape(B, S, D_SB), o_ca.reshape(B, S, D_CA)], axis=-1)
        x = x + gt1 * (mixed @ w_o[l])
        h = rmsnorm(x, g_norm2[l]) * (1.0 + sc2) + sh2
        x = x + gt2 * (jnp.square(jax.nn.relu(h @ w1[l])) @ w2[l])
    return x
```

```python
import contextlib
import numpy as np
import concourse.bass as bass
import concourse.mybir as mybir
from concourse.bass_utils import run_bass_kernel_spmd

F32 = mybir.dt.float32
BF16 = mybir.dt.bfloat16
AF = mybir.ActivationFunctionType
ALU = mybir.AluOpType
EPS = 1e-6


class Tr:
    LIMIT = 6000

    def __init__(self, nc, stack):
        self.nc = nc
        self.stack = stack
        self.eng = {'pe': nc.tensor, 'act': nc.scalar, 'dve': nc.vector, 'pool': nc.gpsimd, 'sp': nc.sync}
        self.sems = []
        self.esem = {}
        self.known = {k: {} for k in self.eng}
        self.lastw = {}
        self.readers = {}
        self.dsem = {}
        self.dtot = {}
        self.nsem = 0

    def newsem(self):
        s = self.stack.enter_context(self.nc.semaphore(f"s{self.nsem}"))
        self.nsem += 1
        self.sems.append(s)
        return len(self.sems) - 1

    def _deps(self, reads, writes):
        deps = set()
        for k in reads:
            w = self.lastw.get(k)
            if w is not None:
                deps.add(w)
        for k in writes:
            w = self.lastw.get(k)
            if w is not None:
                deps.add(w)
            for r in self.readers.get(k, ()):
                deps.add(r)
        return deps

    def _wait(self, eng, deps):
        need = {}
        for (si, val, src) in deps:
            if src == eng and eng == 'pe':
                continue
            if src == 'dma':
                val = self.dtot[si][1]
            if self.known[eng].get(si, 0) >= val:
                continue
            need[si] = max(need.get(si, 0), val)
        for si, val in need.items():
            self.known[eng][si] = val
            self.eng[eng].wait_ge(self.sems[si], val)

    def _record(self, ev, reads, writes):
        for k in reads:
            self.readers.setdefault(k, []).append(ev)
        for k in writes:
            self.lastw[k] = ev
            self.readers[k] = []

    def op(self, eng, fn, reads=(), writes=()):
        self._wait(eng, self._deps(reads, writes))
        st = self.esem.get(eng)
        if st is None or st[1] >= self.LIMIT:
            st = self.esem[eng] = [self.newsem(), 0]
        st[1] += 1
        fn(self.eng[eng]).then_inc(self.sems[st[0]], 1)
        self._record((st[0], st[1], eng), reads, writes)

    def dma(self, q, out, in_, reads, writes, sname, **kw):
        ds = self.dsem.get(sname)
        if ds is None or ds[1] >= 16 * 1500:
            ds = self.dsem[sname] = [self.newsem(), 0]
            self.dtot[ds[0]] = ds
        self._wait(q, [d for d in self._deps(reads, writes) if d[0] != ds[0]])
        ds[1] += 16
        self.eng[q].dma_start(out=out, in_=in_, **kw).then_inc(self.sems[ds[0]], 16)
        self._record((ds[0], ds[1], 'dma'), reads, writes)

    def barrier(self):
        evs = set()
        for e, st in self.esem.items():
            if st[1] > 0:
                evs.add((st[0], st[1], e))
        for n, ds in self.dsem.items():
            evs.add((ds[0], ds[1], 'dma'))
        for k, w in self.lastw.items():
            evs.add(w)
        for k, rs in self.readers.items():
            for r in rs:
                evs.add(r)
        for e in self.eng:
            need = {}
            for (si, val, src) in evs:
                if src == e:
                    continue
                if src == 'dma':
                    val = self.dtot[si][1]
                if self.known[e].get(si, 0) >= val:
                    continue
                need[si] = max(need.get(si, 0), val)
            for si, val in need.items():
                self.known[e][si] = val
                self.eng[e].wait_ge(self.sems[si], val)
        self.lastw = {}
        self.readers = {}


def build():
    nc = bass.Bass("TRN2", target_bir_lowering=False)
    NL = 2

    def din(name, shape, dtype):
        return nc.dram_tensor(name, shape, dtype, kind="ExternalInput").ap()

    def dout(name, shape, dtype):
        return nc.dram_tensor(name, shape, dtype, kind="ExternalOutput").ap()

    def dint(name, shape, dtype):
        return nc.dram_tensor(name, shape, dtype, kind="Internal").ap()

    xT_d = din("xT", [8, 128, 4096], F32)
    cT_d = din("cT", [128, 8], F32)
    wada_d = din("w_ada", [NL, 1024, 6144], F32)
    bada_d = din("b_ada", [NL, 128, 48], F32)
    consts_d = din("consts", [128, 4, 128], F32)
    g1_d = din("g1", [NL, 128, 8], F32)
    win_d = din("w_in", [NL, 1024, 3072], F32)
    gqk_d = din("gqk", [NL, 128, 2], F32)
    g2_d = din("g2", [NL, 128, 8], F32)
    wo_d = din("w_o", [NL, 1024, 1024], F32)
    w1_d = din("w1", [NL, 1024, 4096], F32)
    w2_d = din("w2", [NL, 4096, 1024], F32)
    sbmask_d = din("sbmask", [128, 12, 512], F32)
    camask_d = din("camask", [2, 128, 1920], F32)
    strips_d = din("strips", [NL, 8, 128, 1920], F32)
    blend_d = din("blend", [128, 2], F32)
    out_d = dout("out", [8, 128, 2048], F32)
    qT_d = dint("qT", [8, 128, 2048], BF16)
    kT_d = [dint(f"kT{l}", [8, 128, 4096], BF16) for l in range(NL)]
    v_d = [dint(f"v{l}", [4096, 1024], BF16) for l in range(NL)]
    x1o_d = dint("x1own", [8, 128, 2048], F32)

    with contextlib.ExitStack() as gs:
        T = Tr(nc, gs)

        uid = [0]

        def sb(stack, name, shape, dtype):
            uid[0] += 1
            return stack.enter_context(nc.sbuf_tensor(f"{name}_u{uid[0]}", shape, dtype))

        psall = gs.enter_context(nc.psum_tensor("psall", [128, 8, 512], F32))
        ps = [psall[:, i, :] for i in range(8)]
        xT = sb(gs, "xT_sb", [128, 8, 2048], F32)
        actT = sb(gs, "actT", [128, 8, 2048], BF16)
        cb = sb(gs, "cb", [128, 4, 128], BF16)
        ones64 = sb(gs, "ones64", [128, 64], BF16)
        modvs = [sb(gs, f"modv{l}", [128, 48], F32) for l in range(NL)]
        gsc1s = [sb(gs, f"gsc1_{l}", [128, 8], F32) for l in range(NL)]
        gsc2s = [sb(gs, f"gsc2_{l}", [128, 8], F32) for l in range(NL)]
        gqks = [sb(gs, f"gqk{l}", [128, 2], F32) for l in range(NL)]
        cT = sb(gs, "cT_sb", [128, 8], F32)
        cact = sb(gs, "cact", [128, 8], F32)
        badas = [sb(gs, f"bada{l}", [128, 48], F32) for l in range(NL)]
        blend = sb(gs, "blend_sb", [128, 2], F32)
        sbmask = sb(gs, "sbmask_sb", [128, 12, 512], BF16)
        camask = sb(gs, "camask_sb", [128, 2, 1920], BF16)
        epsb = sb(gs, "epsb", [128, 1], F32)

        ones_f = cb[:, 0, :]
        blk64_f = cb[:, 1, :]
        negtri = cb[:, 2, :]
        negrest = cb[:, 3, :]

        T.dma('pool', cb[:], consts_d[:, :, :], [], ['cb'], 'c1')
        T.dma('sp', cT[:], cT_d[:, :], [], ['cT'], 'c2')
        T.dma('sp', blend[:], blend_d[:, :], [], ['blend'], 'c8')
        T.op('dve', lambda e: e.memset(ones64[:], 1.0), [], ['ones64'])
        T.op('dve', lambda e: e.memset(epsb[:], EPS), [], ['epsb'])
        for i in range(3):
            T.dma('pool', sbmask[:, 4 * i:4 * i + 4, :], sbmask_d[:, 4 * i:4 * i + 4, :], [], [('sbmask', i)], 'c3')
        for i in range(2):
            T.dma('pool', camask[:, i, :], camask_d[i], [], [('camask', i)], 'c4')
        T.op('act', lambda e: e.activation(out=cact[:], in_=cT[:], func=AF.Silu), ['cT'], ['cact'])
        g1s = [sb(gs, f"g1s{l}", [128, 8], F32) for l in range(NL)]
        g2s = [sb(gs, f"g2s{l}", [128, 8], F32) for l in range(NL)]
        for l in range(NL):
            T.dma('sp', badas[l][:], bada_d[l], [], [('bada', l)], 'c5')
            T.dma('sp', g1s[l][:], g1_d[l], [], [('g1', l)], 'c6')
            T.dma('sp', g2s[l][:], g2_d[l], [], [('g2', l)], 'c9')
            T.dma('sp', gqks[l][:], gqk_d[l], [], [('gqk', l)], 'c7')

        def load_x(G, q='sp'):
            for kc in range(8):
                T.dma(q, xT[:, kc, :], xT_d[kc, :, 2048 * G:2048 * G + 2048], [], [('xT', j, kc) for j in range(4)], f'xl{kc % 2}')

        class ModStream:
            def __init__(self, l, chunks, st):
                self.l = l
                self.chunks = list(chunks)
                self.wa = [sb(st, f"wa{i}", [128, 8, 256], F32) for i in range(2)]
                self.src = wada_d[l].rearrange("(kc p) f -> p kc f", p=128)
                self.i = 0
                self._ld(0)

            def _ld(self, i):
                if i < len(self.chunks):
                    ch = self.chunks[i]
                    T.dma('sp', self.wa[i % 2][:], self.src[:, :, ch * 256:(ch + 1) * 256], [], [f'wa{i % 2}'], f'wa{i % 2}')

            @property
            def done(self):
                return self.i >= len(self.chunks)

            def step(self):
                if self.done:
                    return
                i = self.i
                self._ld(i + 1)
                ch = self.chunks[i]
                w = self.wa[i % 2]
                for fi in range(2):
                    fc = ch * 2 + fi
                    for kc in range(8):
                        T.op('pe', lambda e, fc=fc, fi=fi, kc=kc: e.matmul(
                            ps[7][:, fc:fc + 1], lhsT=w[:, kc, fi * 128:(fi + 1) * 128],
                            rhs=cact[:, kc:kc + 1], start=(kc == 0), stop=(kc == 7)),
                            [f'wa{i % 2}', 'cact'], ['ps7'])
                self.i += 1

            def drain(self):
                while not self.done:
                    self.step()

        def mod_fin(l, lo, hi):
            modv = modvs[l]
            T.op('dve', lambda e: e.tensor_tensor(out=modv[:, lo:hi], in0=ps[7][:, lo:hi], in1=badas[l][:, lo:hi], op=ALU.add),
                 ['ps7', ('bada', l)], [('modv', l, lo)])
            if lo <= 8 and hi >= 16:
                T.op('dve', lambda e: e.scalar_tensor_tensor(out=gsc1s[l][:], in0=modv[:, 8:16], scalar=1.0, in1=g1s[l][:],
                                                             op0=ALU.add, op1=ALU.mult), [('modv', l, lo), ('g1', l)], [('gsc1', l)])
            if lo <= 32 and hi >= 40:
                T.op('dve', lambda e: e.scalar_tensor_tensor(out=gsc2s[l][:], in0=modv[:, 32:40], scalar=1.0, in1=g2s[l][:],
                                                             op0=ALU.add, op1=ALU.mult), [('modv', l, lo), ('g2', l)], [('gsc2', l)])

        def emit_ln(st_tiles, j, gsc, modv, shoff):
            sqs, lnv, rstd, tmps = st_tiles
            tl = slice(j * 512, (j + 1) * 512)
            for kc in range(8):
                T.op('act', lambda e, kc=kc: e.activation(out=sqs[:, kc, :], in_=xT[:, kc, tl], func=AF.Square),
                     [('xT', j, kc)], [('sq', kc)])
            for kc in range(8):
                T.op('pe', lambda e, kc=kc: e.matmul(ps[4][:, :], lhsT=ones_f, rhs=sqs[:, kc, :], start=(kc == 0), stop=(kc == 7)),
                     [('sq', kc)], ['ps4'])
            T.op('act', lambda e: e.activation(out=lnv[:], in_=ps[4][:, :], func=AF.Ln, scale=1.0 / 1024, bias=epsb[:, 0:1]),
                 ['ps4', 'epsb'], ['lnv'])
            T.op('act', lambda e: e.activation(out=rstd[:], in_=lnv[:], func=AF.Exp, scale=-0.5), ['lnv'], ['rstd'])
            for kc in range(8):
                tm = tmps[kc % 2]
                T.op('dve', lambda e, kc=kc, tm=tm: e.scalar_tensor_tensor(
                    out=tm[:], in0=xT[:, kc, tl], scalar=gsc[:, kc:kc + 1], in1=rstd[:], op0=ALU.mult, op1=ALU.mult),
                    [('xT', j, kc), 'rstd', 'gsc'], [f'lt{kc % 2}'])
                T.op('act', lambda e, kc=kc, tm=tm: e.activation(
                    out=actT[:, kc, tl], in_=tm[:], func=AF.Identity, bias=modv[:, shoff + kc:shoff + kc + 1], scale=1.0),
                    [f'lt{kc % 2}', 'modv'], [('actT', j)])

        def ln_tiles(st):
            return (sb(st, "sq8", [128, 8, 512], BF16), sb(st, "lnv", [128, 512], F32),
                    sb(st, "rstd", [128, 512], F32), [sb(st, f"lt{i}", [128, 512], F32) for i in range(2)])

        blst = [None]

        def save_own(G, jj):
            stg = blst[0]
            ta, tb_ = 2 * jj, 2 * jj + 1
            for kc in range(8):
                T.op('dve', lambda e, kc=kc: e.tensor_scalar(out=stg[:, kc, :], in0=xT[:, kc, ta * 512:(ta + 1) * 512], scalar1=blend[:, 0:1],
                                                             scalar2=None, op0=ALU.mult), [('xT', ta, kc)], [('blst', kc)])
                T.op('dve', lambda e, kc=kc: e.scalar_tensor_tensor(out=stg[:, kc, :], in0=xT[:, kc, tb_ * 512:(tb_ + 1) * 512], scalar=blend[:, 1:2],
                                                                    in1=stg[:, kc, :], op0=ALU.mult, op1=ALU.add),
                     [('xT', tb_, kc), ('blst', kc)], [('blst', kc)])
            j = 2 * G + jj
            for kc in range(8):
                T.dma('sp', x1o_d[kc, :, j * 512:(j + 1) * 512], stg[:, kc, :], [('blst', kc)], [('x1own', j)], 'xs0')

        def load_own():
            for kc in range(8):
                T.dma('sp', xT[:, kc, :], x1o_d[kc], [('x1own', j) for j in range(4)], [('xT', j, kc) for j in range(4)], f'xl{kc % 2}')

        def phase_A(l, cgs, G, modspec=None, after_ln=None, mid=None, mid1=None):
            modv = modvs[l]
            gqk = gqks[l]
            with contextlib.ExitStack() as st:
                lt = ln_tiles(st)
                wq = [sb(st, f"wq{i}", [128, 8, 512], BF16) for i in range(2)]
                stage = [sb(st, f"stg{i}", [128, 2048], BF16) for i in range(4)]
                vst = [sb(st, f"vst{i}", [128, 512], BF16) for i in range(4)]
                sq2s = [sb(st, f"sq2_{i}", [128, 512], BF16) for i in range(2)]
                lnv2s = [sb(st, f"lnv2_{i}", [128, 512], F32) for i in range(2)]
                rstd2s = [sb(st, f"rstd2_{i}", [128, 512], F32) for i in range(2)]
                ncnt = [0]
                pending = []
                if after_ln is not None:
                    blst[0] = sb(st, "blst", [128, 8, 512], F32)
                wsrc = win_d[l].rearrange("(kc p) f -> p kc f", p=128)
                ms = ModStream(modspec[0], modspec[1], st) if modspec else None

                def ld(i):
                    cg = cgs[i]
                    T.dma('pool', wq[i % 2][:], wsrc[:, :, cg * 512:(cg + 1) * 512], [], [f'wq{i % 2}'], f'wq{i % 2}')
                ld(0)
                cnt = 0
                scnt = 0
                first_is_qk = cgs[0] in (0, 1, 3, 4)
                if not first_is_qk:
                    for j in range(4):
                        emit_ln(lt, j, gsc1s[l], modv, 0)
                    if after_ln:
                        after_ln()
                for ci, cg in enumerate(cgs):
                    if ci + 1 < len(cgs):
                        ld(ci + 1)
                    if ci == 1 and mid1:
                        mid1()
                    if ci == 2 and mid:
                        mid()
                    w = wq[ci % 2]
                    wk = f'wq{ci % 2}'
                    if cg in (0, 1, 3, 4):
                        is_ca = cg >= 3
                        is_q = cg in (0, 3)
                        tile_major = (ci == 0)
                        order = [(pp, j) for j in range(4) for pp in range(4)] if tile_major else [(pp, j) for pp in range(4) for j in range(4)]
                        for oi, (pp, j) in enumerate(order):
                            if tile_major and pp == 0:
                                if j == 0:
                                    emit_ln(lt, 0, gsc1s[l], modv, 0)
                                if j + 1 < 4:
                                    emit_ln(lt, j + 1, gsc1s[l], modv, 0)
                                    if j + 1 == 3 and after_ln:
                                        after_ln()
                            hp = pp + (4 if is_ca else 0)
                            if tile_major:
                                sg = stage[pp]
                                sk = f'stg{pp}'
                            else:
                                if j == 0:
                                    scnt += 1
                                sg = stage[scnt % 2]
                                sk = f'stg{scnt % 2}'
                            if True:
                                p_ = ps[cnt % 4]
                                pk = f'ps{cnt % 4}'
                                cnt += 1
                                tl = slice(j * 512, (j + 1) * 512)
                                for kc in range(8):
                                    T.op('pe', lambda e, kc=kc, p_=p_, w=w, pp=pp, tl=tl: e.matmul(
                                        p_[:, :], lhsT=w[:, kc, pp * 128:(pp + 1) * 128], rhs=actT[:, kc, tl],
                                        start=(kc == 0), stop=(kc == 7)), [wk, ('actT', j)], [pk])
                                if not is_ca:
                                    if cnt % 2 == 0:
                                        T.op('act', lambda e, p_=p_, sg=sg, tl=tl: e.activation(out=sg[:, tl], in_=p_[:, :], func=AF.Copy),
                                             [pk], [(sk, j)])
                                    else:
                                        T.op('dve', lambda e, p_=p_, sg=sg, tl=tl: e.tensor_copy(out=sg[:, tl], in_=p_[:, :]), [pk], [(sk, j)])
                                else:
                                    gi = 0 if is_q else 1
                                    ni = ncnt[0] % 2
                                    ncnt[0] += 1
                                    sq2, lnv2, rstd2, pn, pnk = sq2s[ni], lnv2s[ni], rstd2s[ni], ps[5 + ni], f'ps{5 + ni}'
                                    T.op('act', lambda e, p_=p_, sq2=sq2: e.activation(out=sq2[:], in_=p_[:, :], func=AF.Square), [pk], [f'sq2_{ni}'])

                                    def norm_tail(p_=p_, pk=pk, sg=sg, sk=sk, tl=tl, gi=gi, ni=ni, sq2=sq2, lnv2=lnv2, rstd2=rstd2, pn=pn, pnk=pnk, j=j):
                                        T.op('pe', lambda e: e.matmul(pn[:, :], lhsT=blk64_f, rhs=sq2[:], start=True, stop=True),
                                             [f'sq2_{ni}'], [pnk])
                                        T.op('act', lambda e: e.activation(out=lnv2[:], in_=pn[:, :], func=AF.Ln, scale=1.0 / 64,
                                                                           bias=epsb[:, 0:1]), [pnk, 'epsb'], [f'lnv2_{ni}'])
                                        T.op('act', lambda e: e.activation(out=rstd2[:], in_=lnv2[:], func=AF.Exp, scale=-0.5),
                                             [f'lnv2_{ni}'], [f'rstd2_{ni}'])
                                        T.op('dve', lambda e: e.scalar_tensor_tensor(
                                            out=sg[:, tl], in0=p_[:, :], scalar=gqk[:, gi:gi + 1], in1=rstd2[:], op0=ALU.mult, op1=ALU.mult),
                                            [pk, f'rstd2_{ni}', 'gqk'], [(sk, j)])
                                    prev = pending[:]
                                    del pending[:]
                                    for f_ in prev:
                                        f_()
                                    pending.append(norm_tail)
                            last_of_pp = (j == 3) if not tile_major else (oi >= 12)
                            if last_of_pp:
                                def store(sg=sg, sk=sk, hp=hp, is_q=is_q, sn=f'st{pp if tile_major else scnt % 2}'):
                                    dst = qT_d[hp] if is_q else kT_d[l][hp, :, 2048 * G:2048 * G + 2048]
                                    T.dma('sp', dst, sg[:], [(sk, jj) for jj in range(4)], [('qkd', is_q, hp)], sn)
                                    if ms:
                                        ms.step()
                                if is_ca:
                                    pending.append(store)
                                else:
                                    store()
                        for f_ in pending:
                            f_()
                        del pending[:]
                    else:
                        voff = 0 if cg == 2 else 512
                        for tb in range(16):
                            p_ = ps[cnt % 4]
                            pk = f'ps{cnt % 4}'
                            cnt += 1
                            for kc in range(8):
                                T.op('pe', lambda e, kc=kc, p_=p_, w=w, tb=tb: e.matmul(
                                    p_[:, :], lhsT=actT[:, kc, tb * 128:(tb + 1) * 128], rhs=w[:, kc, :],
                                    start=(kc == 0), stop=(kc == 7)), [wk, ('actT', tb // 4)], [pk])
                            vs = vst[tb % 4]
                            vk = f'vst{tb % 4}'
                            if tb % 2 == 0:
                                T.op('act', lambda e, p_=p_, vs=vs: e.activation(out=vs[:], in_=p_[:, :], func=AF.Copy), [pk], [vk])
                            else:
                                T.op('dve', lambda e, p_=p_, vs=vs: e.tensor_copy(out=vs[:], in_=p_[:, :]), [pk], [vk])
                            r0 = 2048 * G + tb * 128
                            T.dma('sp', v_d[l][r0:r0 + 128, voff:voff + 512], vs[:], [vk], [('v_d', tb, cg)], f'vs{tb % 4}')
                            if ms and tb % 4 == 3:
                                ms.step()
                if ms:
                    ms.drain()
                    mod_fin(modspec[0], modspec[2], modspec[3])
                T.barrier()

        def phase_attn(l, G):
            own = G is None
            nkt = 8 if own else 4 * (G + 1)
            EW = 1920 if own else 1408
            cmi = 1 if own else 0
            with contextlib.ExitStack() as st:
                kTs = [sb(st, f"kTs{i}", [128, 4096], BF16) for i in range(2)]
                vs_ = [sb(st, f"vsb{i}", [128, 32, 128], BF16) for i in range(2)]
                qs = [sb(st, f"qs{i}", [128, 2048], BF16) for i in range(2)]
                ebs = [sb(st, f"eb{i}", [128, 1920], F32) for i in range(2)]
                ebuf = [sb(st, f"e{i}", [128, 2, 512], F32) for i in range(3)]
                Lb = [sb(st, f"L{i}", [128, 2, 512], BF16) for i in range(2)]
                xc = [sb(st, f"xc{i}", [128, 2, 512], BF16) for i in range(2)]
                Ab = [sb(st, f"A{i}", [128, 2, 512], BF16) for i in range(2)]
                recs = [sb(st, f"rec{i}", [128, 512], F32) for i in range(2)]
                osb = [sb(st, f"osb{i}", [128, 512], F32) for i in range(2)]

                def load_hp(hp):
                    s = hp % 2
                    T.dma('sp', qs[s][:], qT_d[hp], [], [f'qs{s}'], f'lq{s}')
                    T.dma('sp', kTs[s][:, 0:nkt * 512], kT_d[l][hp, :, 0:nkt * 512], [], [f'kTs{s}'], f'lk{s}')
                    for k2 in range(0, nkt, 2):
                        srcv = v_d[l][k2 * 512:(k2 + 2) * 512, hp * 128:(hp + 1) * 128].rearrange("(n p) c -> p n c", p=128)
                        T.dma('sp', vs_[s][:, k2 * 4:(k2 + 2) * 4, :], srcv, [], [f'vsb{s}'], f'lv{s}')

                def prep_eb_dma(hc):
                    s = hc % 2
                    T.dma('sp', ebs[s][:, 0:EW], strips_d[l, hc, :, 0:EW], [], [f'eb{s}'], f'leb{s}')

                def prep_eb(hc):
                    s = hc % 2
                    T.op('act', lambda e: e.activation(out=ebs[s][:, 0:EW], in_=ebs[s][:, 0:EW], func=AF.Exp), [f'eb{s}'], [f'eb{s}'])
                    T.op('dve', lambda e: e.tensor_tensor(out=ebs[s][:, 0:EW], in0=ebs[s][:, 0:EW], in1=camask[:, cmi, 0:EW], op=ALU.mult),
                         [f'eb{s}'], [f'eb{s}'])

                def run_sb_all():
                    ul = []
                    hp_first = {}
                    for hp in range(4):
                      hp_first[len(ul)] = hp
                      for j in range(4):
                        if own:
                            nb = 4 * (2 * j + 2)
                        else:
                            nb = 4 * (4 * G + j + 1)
                        for kb in range(nb - 1, -1, -1):
                            kt, b = kb // 4, kb % 4
                            m = None
                            c0 = 0
                            if own:
                                if kt == 2 * j:
                                    m = 4 + b
                                elif kt == 2 * j + 1:
                                    m = 8 + b
                                    c0 = 128 * b
                            elif kt == 4 * G + j:
                                m = b
                                c0 = 128 * b
                            ul.append(dict(hp=hp, j=j, kb=kb, first=(kb == nb - 1), last=(kb == 0), m=m, c0=c0))
                    n = len(ul)

                    def bufs(u):
                        s = u['hp'] % 2
                        return kTs[s], vs_[s], qs[s], f'kTs{s}', f'vsb{s}', f'qs{s}'
                    hs = [slice(0, 64), slice(64, 128)]

                    def zi(c, i):
                        return 2 * (i % 2) + c

                    def QK(c, i):
                        u = ul[i]
                        K, V, Q, kk, vk, qk = bufs(u)
                        c0 = u['c0']
                        bi = zi(c, i)
                        T.op('pe', lambda e: e.matmul(ps[bi][:, c0:512], lhsT=K[hs[c], u['kb'] * 128:(u['kb'] + 1) * 128],
                                                      rhs=Q[hs[c], u['j'] * 512 + c0:(u['j'] + 1) * 512], start=True, stop=True),
                             [kk, qk], [f'ps{bi}'])

                    def EXP2(i):
                        c0 = ul[i]['c0']
                        p = 2 * (i % 2)
                        T.op('act', lambda e: e.activation(out=ebuf[i % 3][:, :, c0:512], in_=psall[:, p:p + 2, c0:512], func=AF.Exp, scale=0.125),
                             [f'ps{p}', f'ps{p + 1}'], [f'e{i % 3}'])

                    def MASK(c, i):
                        u = ul[i]
                        if u['m'] is None:
                            return
                        c0 = u['c0']
                        eb_ = ebuf[i % 3]
                        T.op('dve', lambda e: e.tensor_tensor(out=eb_[:, c, c0:512], in0=eb_[:, c, c0:512], in1=sbmask[:, u['m'], c0:512], op=ALU.mult),
                             [f'e{i % 3}'], [f'e{i % 3}'])

                    def LN2(i):
                        c0 = ul[i]['c0']
                        T.op('act', lambda e: e.activation(out=Lb[i % 2][:, :, c0:512], in_=ebuf[i % 3][:, :, c0:512], func=AF.Ln, bias=1.0, scale=1.0),
                             [f'e{i % 3}'], [f'L{i % 2}'])

                    def TRI(c, i):
                        u = ul[i]
                        c0 = u['c0']
                        T.op('pe', lambda e: e.matmul(ps[4 + c][:, c0:512], lhsT=negtri, rhs=Lb[i % 2][:, c, c0:512], start=u['first'], stop=True,
                                                      skip_group_check=True),
                             [f'L{i % 2}'], [f'ps{4 + c}'])

                    def EXPC2(i):
                        c0 = ul[i]['c0']
                        T.op('act', lambda e: e.activation(out=xc[i % 2][:, :, c0:512], in_=psall[:, 4:6, c0:512], func=AF.Exp),
                             ['ps4', 'ps5'], [f'xc{i % 2}'])

                    def REST(c, i):
                        u = ul[i]
                        if u['last']:
                            return
                        c0 = u['c0']
                        T.op('pe', lambda e: e.matmul(ps[4 + c][:, c0:512], lhsT=negrest, rhs=Lb[i % 2][:, c, c0:512], start=False, stop=True,
                                                      skip_group_check=True),
                             [f'L{i % 2}'], [f'ps{4 + c}'])

                    def MULT2(i):
                        c0 = ul[i]['c0']
                        T.op('dve', lambda e: e.tensor_tensor(out=Ab[i % 2][:, :, c0:512], in0=ebuf[i % 3][:, :, c0:512], in1=xc[i % 2][:, :, c0:512], op=ALU.mult),
                             [f'e{i % 3}', f'xc{i % 2}'], [f'A{i % 2}'])

                    def PV(c, i):
                        u = ul[i]
                        K, V, Q, kk, vk, qk = bufs(u)
                        hp = u['hp']
                        c0 = u['c0']
                        ob = ps[6 + u['j'] % 2]
                        ok = (f"ps{6 + u['j'] % 2}", c)
                        T.op('pe', lambda e: e.matmul(ob[hs[c], c0:512], lhsT=V[:, u['kb'], c * 64:(c + 1) * 64], rhs=Ab[i % 2][:, c, c0:512],
                                                      start=u['first'], stop=u['last'], skip_group_check=True), [vk, f'A{i % 2}'], [ok])
                        if u['last']:
                            tl = slice(u['j'] * 512, (u['j'] + 1) * 512)
                            if c == 0:
                                T.op('dve', lambda e: e.tensor_copy(out=actT[hs[c], hp, tl], in_=ob[hs[c], :]), [ok], [('actT', u['j'], hp, c)])
                            else:
                                T.op('act', lambda e: e.activation(out=actT[hs[c], hp, tl], in_=ob[hs[c], :], func=AF.Copy), [ok], [('actT', u['j'], hp, c)])

                    for c in range(2):
                        QK(c, 0)
                    for s_ in range(n + 2):
                        if (s_ - 3) in hp_first:
                            nh = hp_first[s_ - 3] + 1
                            load_hp(nh)
                            if nh == 4:
                                prep_eb_dma(0)
                        if (s_ - 9) in hp_first and hp_first[s_ - 9] == 3:
                            prep_eb(0)
                        if 0 <= s_ - 1 < n:
                            for c in range(2):
                                TRI(c, s_ - 1)
                        if s_ + 1 < n:
                            for c in range(2):
                                QK(c, s_ + 1)
                        if s_ < n:
                            EXP2(s_)
                            for c in range(2):
                                MASK(c, s_)
                        if 0 <= s_ - 1 < n:
                            EXPC2(s_ - 1)
                        if s_ < n:
                            LN2(s_)
                        if 0 <= s_ - 2 < n:
                            for c in range(2):
                                PV(c, s_ - 2)
                        if 0 <= s_ - 1 < n:
                            for c in range(2):
                                REST(c, s_ - 1)
                            MULT2(s_ - 1)

                def run_ca_all():
                    def crange(r):
                        los, his = [], []
                        for sh in ((0, 8) if own else (0,)):
                            lo, hi = max(0, 2 * r - 8 - sh), min(7, 2 * r + 1 - sh)
                            if lo <= hi:
                                los.append(lo)
                                his.append(hi)
                        return 64 * min(los), 64 * (max(his) + 1)

                    chains = [[], []]
                    head_first = {}
                    for hp in range(4, 8):
                        for hh in range(2):
                            hc = (hp - 4) * 2 + hh
                            head_first[len(chains[0])] = (hc, hp, hh)
                            seg = [[], []]
                            for i2 in range(2):
                                for c in range(2):
                                    j = 2 * i2 + c
                                    if own:
                                        r0 = 4 if j == 0 else 0
                                        seg[c] += [dict(hp=hp, hh=hh, hc=hc, j=j, kb=8 * j - 4 + r, off=128 * (11 - r), first=(r == r0),
                                                        last=(r == 11), cr=crange(r)) for r in range(r0, 12)]
                                    else:
                                        Tg = 4 * G + j
                                        r0 = 4 if Tg == 0 else 0
                                        seg[c] += [dict(hp=hp, hh=hh, hc=hc, j=j, kb=4 * Tg - 4 + r, off=128 * (7 - r), first=(r == r0),
                                                        last=(r == 7), cr=crange(r)) for r in range(r0, 8)]
                            m_ = max(len(seg[0]), len(seg[1]))
                            for c in range(2):
                                chains[c] += seg[c] + [None] * (m_ - len(seg[c]))
                    n = len(chains[0])

                    def unit(c, i):
                        return chains[c][i] if 0 <= i < n else None

                    def zb(c, i):
                        return ps[c * 2 + i % 2], f'ps{c * 2 + i % 2}'

                    def QK(c, i):
                        u = unit(c, i)
                        if u is None:
                            return
                        s = u['hp'] % 2
                        hsl = slice(u['hh'] * 64, (u['hh'] + 1) * 64)
                        p_, pk = zb(c, i)
                        a0, a1 = u['cr']
                        T.op('pe', lambda e: e.matmul(p_[:, a0:a1], lhsT=kTs[s][hsl, u['kb'] * 128:(u['kb'] + 1) * 128],
                                                      rhs=qs[s][hsl, u['j'] * 512 + a0:u['j'] * 512 + a1], start=True, stop=True),
                             [f'kTs{s}', f'qs{s}'], [pk])

                    def EXP(c, i):
                        u = unit(c, i)
                        if u is None:
                            return
                        p_, pk = zb(c, i)
                        a0, a1 = u['cr']
                        T.op('act', lambda e: e.activation(out=ebuf[i % 3][:, c, a0:a1], in_=p_[:, a0:a1], func=AF.Exp, scale=0.125), [pk], [(f'e{i % 3}', c)])

                    def MULT(c, i):
                        u = unit(c, i)
                        if u is None:
                            return
                        off = u['off']
                        a0, a1 = u['cr']
                        EB = ebs[u['hc'] % 2]
                        T.op('dve', lambda e: e.tensor_tensor(out=Ab[i % 2][:, c, a0:a1], in0=ebuf[i % 3][:, c, a0:a1], in1=EB[:, off + a0:off + a1], op=ALU.mult),
                             [(f'e{i % 3}', c), f"eb{u['hc'] % 2}"], [(f'A{i % 2}', c)])

                    def PV(c, i):
                        u = unit(c, i)
                        if u is None:
                            return
                        hp, hh = u['hp'], u['hh']
                        s = hp % 2
                        V = vs_[s]
                        vk = f'vsb{s}'
                        hsl = slice(hh * 64, (hh + 1) * 64)
                        ob = ps[6 + c]
                        db = ps[4 + c]
                        ok = (f'ps{6 + c}', hh)
                        dk = (f'ps{4 + c}', hh)
                        a0, a1 = u['cr']
                        T.op('pe', lambda e: e.matmul(ob[hsl, a0:a1], lhsT=V[:, u['kb'], hh * 64:(hh + 1) * 64], rhs=Ab[i % 2][:, c, a0:a1],
                                                      start=u['first'], stop=u['last'], skip_group_check=True), [vk, (f'A{i % 2}', c)], [ok])
                        T.op('pe', lambda e: e.matmul(db[hsl, a0:a1], lhsT=ones64[:, :], rhs=Ab[i % 2][:, c, a0:a1],
                                                      start=u['first'], stop=u['last'], skip_group_check=True), [(f'A{i % 2}', c)], [dk])
                        if u['last']:
                            tl = slice(u['j'] * 512, (u['j'] + 1) * 512)
                            rc = recs[c]
                            oc = osb[c]
                            T.op('act', lambda e: e.activation(out=rc[hsl, :], in_=db[hsl, :], func=AF.Ln), [dk], [f'rec{c}'])
                            T.op('dve', lambda e: e.tensor_copy(out=oc[hsl, :], in_=ob[hsl, :]), [ok], [f'osb{c}'])
                            T.op('act', lambda e: e.activation(out=rc[hsl, :], in_=rc[hsl, :], func=AF.Exp, scale=-1.0), [f'rec{c}'], [f'rec{c}'])
                            T.op('dve', lambda e: e.tensor_tensor(out=actT[hsl, hp, tl], in0=oc[hsl, :], in1=rc[hsl, :], op=ALU.mult),
                                 [f'osb{c}', f'rec{c}'], [('actT', u['j'], hp, hh)])

                    for c in range(2):
                        QK(c, 0)
                    for s_ in range(n + 1):
                        if (s_ - 2) in head_first:
                            hc, hp, hh = head_first[s_ - 2]
                            if hc + 1 < 8:
                                prep_eb_dma(hc + 1)
                            if hh == 0 and hp + 1 < 8:
                                load_hp(hp + 1)
                        if (s_ - 8) in head_first:
                            hc, hp, hh = head_first[s_ - 8]
                            if hc + 1 < 8:
                                prep_eb(hc + 1)
                        for c in range(2):
                            QK(c, s_ + 1)
                        for c in range(2):
                            EXP(c, s_)
                        for c in range(2):
                            MULT(c, s_)
                        for c in range(2):
                            PV(c, s_ - 1)

                load_hp(0)
                run_sb_all()
                T.barrier()
                run_ca_all()
                T.barrier()

        def phase_wo(l):
            modv = modvs[l]
            with contextlib.ExitStack() as st:
                wo = sb(st, "wo_sb", [128, 8, 1024], BF16)
                wsrc = wo_d[l].rearrange("(kc p) f -> p kc f", p=128)
                for h in range(2):
                    T.dma('pool', wo[:, :, h * 512:(h + 1) * 512], wsrc[:, :, h * 512:(h + 1) * 512], [], [('wo', h)], f'wo{h}')
                cnt = 0
                for j in range(4):
                    tl = slice(j * 512, (j + 1) * 512)
                    for d in range(8):
                        p_ = ps[cnt % 4]
                        pk = f'ps{cnt % 4}'
                        cnt += 1
                        for hp in range(8):
                            T.op('pe', lambda e, hp=hp, p_=p_, d=d, tl=tl: e.matmul(
                                p_[:, :], lhsT=wo[:, hp, d * 128:(d + 1) * 128], rhs=actT[:, hp, tl], start=(hp == 0), stop=(hp == 7)),
                                [('wo', d // 4)], [pk])
                        T.op('dve', lambda e, p_=p_, d=d, tl=tl: e.scalar_tensor_tensor(
                            out=xT[:, d, tl], in0=p_[:, :], scalar=modv[:, 16 + d:17 + d], in1=xT[:, d, tl], op0=ALU.mult, op1=ALU.add),
                            [pk, ('xT', j, d)], [('xT', j, d)])
                T.barrier()

        def phase_mlp(l, modspec=None, final=False):
            modv = modvs[l]
            with contextlib.ExitStack() as st:
                ms = ModStream(modspec[0], modspec[1], st) if modspec else None
                lt = ln_tiles(st)
                w1s = [sb(st, f"w1s{i}", [128, 8, 512], BF16) for i in range(2)]
                w2s = [sb(st, f"w2s{i}", [128, 4, 1024], BF16) for i in range(2)]
                h1 = [sb(st, f"h1_{i}", [128, 4, 512], BF16) for i in range(2)]
                rt = [sb(st, f"rt{i}", [128, 512], F32) for i in range(2)]
                w1src = w1_d[l].rearrange("(kc p) f -> p kc f", p=128)

                def ld(fg):
                    s = fg % 2
                    T.dma('pool', w1s[s][:], w1src[:, :, fg * 512:(fg + 1) * 512], [], [f'w1s{s}'], f'w1s{s}')
                    T.dma('pool', w2s[s][:], w2_d[l, fg * 512:(fg + 1) * 512, :].rearrange("(fi p) d -> p fi d", p=128), [], [f'w2s{s}'], f'w2s{s}')
                ld(0)
                cnts = dict(c1=0, c2=0)

                def stage1(fg, j, hb, hk):
                    s = fg % 2
                    tl = slice(j * 512, (j + 1) * 512)
                    for fi in range(4):
                        c1 = cnts['c1']
                        cnts['c1'] += 1
                        p_ = ps[c1 % 4]
                        pk = f'ps{c1 % 4}'
                        r_ = rt[c1 % 2]
                        rk = f'rt{c1 % 2}'
                        for kc in range(8):
                            T.op('pe', lambda e, kc=kc: e.matmul(
                                p_[:, :], lhsT=w1s[s][:, kc, fi * 128:(fi + 1) * 128], rhs=actT[:, kc, tl], start=(kc == 0), stop=(kc == 7)),
                                [f'w1s{s}', ('actT', j)], [pk])
                        T.op('act', lambda e: e.activation(out=r_[:], in_=p_[:, :], func=AF.Relu), [pk], [rk])
                        T.op('dve', lambda e: e.tensor_tensor(out=hb[:, fi, :], in0=r_[:], in1=r_[:], op=ALU.mult), [rk], [(hk, fi)])

                def stage2(fg, j, hb, hk):
                    s = fg % 2
                    tl = slice(j * 512, (j + 1) * 512)
                    for d in range(8):
                        c2 = cnts['c2']
                        cnts['c2'] += 1
                        p_ = ps[4 + c2 % 3]
                        pk = f'ps{4 + c2 % 3}'
                        for fi in range(4):
                            T.op('pe', lambda e, fi=fi: e.matmul(
                                p_[:, :], lhsT=w2s[s][:, fi, d * 128:(d + 1) * 128], rhs=hb[:, fi, :], start=(fi == 0), stop=(fi == 3)),
                                [f'w2s{s}', (hk, fi)], [pk])
                        T.op('dve', lambda e: e.scalar_tensor_tensor(
                            out=xT[:, d, tl], in0=p_[:, :], scalar=modv[:, 40 + d:41 + d], in1=xT[:, d, tl], op0=ALU.mult, op1=ALU.add),
                            [pk, ('xT', j, d)], [('xT', j, d)])
                    if final and fg == 7:
                        for kc in range(8):
                            T.dma('sp', out_d[kc, :, tl], xT[:, kc, tl], [('xT', j, kc)], [('out', kc, j)], f'xl{kc % 2}')
                    if ms:
                        ms.step()

                its = [(fg, j) for fg in range(8) for j in range(4)]
                prev = None
                for k, (fg, j) in enumerate(its):
                    if fg == 0:
                        if j == 0:
                            emit_ln(lt, 0, gsc2s[l], modv, 24)
                        if j + 1 < 4:
                            emit_ln(lt, j + 1, gsc2s[l], modv, 24)
                    hb = h1[k % 2]
                    hk = f'h1_{k % 2}'
                    stage1(fg, j, hb, hk)
                    if prev is not None:
                        stage2(*prev)
                    if j == 0 and fg + 1 < 8:
                        ld(fg + 1)
                    prev = (fg, j, hb, hk)
                stage2(*prev)
                if ms:
                    ms.drain()
                    mod_fin(modspec[0], modspec[2], modspec[3])
                T.barrier()

        with contextlib.ExitStack() as st0:
            ms0 = ModStream(0, range(0, 8), st0)
            load_x(0, q='pool')
            ms0.drain()
            mod_fin(0, 0, 16)
            T.barrier()
        for G in range(2):
            phase_A(0, [0, 1, 2, 3, 4, 5], G, modspec=(0, range(8, 24), 16, 48) if G == 0 else None)
            phase_attn(0, G)
            phase_wo(0)
            phase_mlp(0, modspec=(1, range(0, 24), 0, 48) if G == 0 else None)
            phase_A(1, [1, 2, 4, 5], G, after_ln=(lambda G=G: save_own(G, 0)), mid1=(lambda G=G: save_own(G, 1)),
                    mid=(lambda: load_x(1)) if G == 0 else load_own)
        phase_A(1, [0, 3], None)
        phase_attn(1, None)
        phase_wo(1)
        phase_mlp(1, final=True)
        T.barrier()
    return nc


def _consts():
    jj = np.arange(128)[:, None]
    ss = np.arange(128)[None, :]
    ones = np.ones((128, 128), np.float32)
    blk = ((jj // 64) == (ss // 64)).astype(np.float32)
    negtri = -(jj >= ss).astype(np.float32)
    negrest = -(jj < ss).astype(np.float32)
    return np.ascontiguousarray(np.stack([ones, blk, negtri, negrest], axis=1))


def _sbmask(g):
    s = np.arange(128)[:, None, None]
    b = np.arange(4)[None, :, None]
    t = np.arange(512)[None, None, :]
    diag = ((128 * b + s) < t).astype(np.float32)
    onesm = np.ones_like(diag)
    zer = np.zeros_like(diag)
    mA, mB = (diag, zer) if g == 0 else (onesm, diag)
    return np.ascontiguousarray(np.concatenate([diag, mA, mB], axis=1))


def _camask(g):
    s = np.arange(128)[:, None]
    c = np.arange(1920)[None, :]
    d0 = c // 64 - s // 64 - 6
    d1 = c // 64 - s // 64 + 8 * g - 14
    m0 = ((d0 >= 0) & (d0 <= 8)).astype(np.float32)
    m1 = ((d1 >= 0) & (d1 <= 8)).astype(np.float32)
    return np.ascontiguousarray(np.stack([m0, m1], axis=0))


def _strips(rel_bias, g):
    s = np.arange(128)[:, None]
    c = np.arange(1920)[None, :]
    idx0 = np.clip(c - s - 384, -128, 128) + 128
    idx1 = np.clip(c - s + 512 * g - 896, -128, 128) + 128
    return np.ascontiguousarray(np.stack([rel_bias[0][:, idx0], rel_bias[1][:, idx1]], axis=0))


def _tok_idx(g):
    return np.concatenate([np.arange(512 * (2 * j + g), 512 * (2 * j + g) + 512) for j in range(4)])


_NC = {}


def kernel(x, c, g_norm1, w_in, g_q, g_k, rel_bias, w_o, g_norm2, w1, w2, w_ada, b_ada):
    x = np.asarray(x, np.float32)
    c = np.asarray(c, np.float32)
    f = lambda a: np.ascontiguousarray(np.asarray(a, np.float32))
    g_norm1, w_in, g_q, g_k, rel_bias, w_o, g_norm2, w1, w2, w_ada, b_ada = map(
        f, (g_norm1, w_in, g_q, g_k, rel_bias, w_o, g_norm2, w1, w2, w_ada, b_ada))
    L = 2
    cores = [(b, g) for b in range(4) for g in range(2)]
    consts = _consts()
    g1l = np.ascontiguousarray(g_norm1.reshape(L, 8, 128).transpose(0, 2, 1))
    g2l = np.ascontiguousarray(g_norm2.reshape(L, 8, 128).transpose(0, 2, 1))
    badal = np.ascontiguousarray(b_ada.reshape(L, 48, 128).transpose(0, 2, 1))
    gqk = np.ascontiguousarray(np.stack([np.tile(g_q, (1, 2)), np.tile(g_k, (1, 2))], axis=2))
    strips = [_strips(rel_bias, g) for g in range(2)]
    sbm = [_sbmask(g) for g in range(2)]
    cam = [_camask(g) for g in range(2)]
    blends = [np.ascontiguousarray(np.tile(np.array([[1.0, 0.0]] if g == 0 else [[0.0, 1.0]], np.float32), (128, 1))) for g in range(2)]
    xTb = [np.ascontiguousarray(x[b].T).reshape(8, 128, 4096) for b in range(4)]
    if 'F' not in _NC:
        _NC['F'] = build()
    nc = _NC['F']
    maps = []
    for (b, g) in cores:
        maps.append(dict(xT=xTb[b], cT=np.ascontiguousarray(c[b].reshape(8, 128).T), w_ada=w_ada, b_ada=badal, consts=consts,
                         g1=g1l, w_in=w_in, gqk=gqk, g2=g2l, w_o=w_o, w1=w1, w2=w2, sbmask=sbm[g], camask=cam[g],
                         strips=strips[g], blend=blends[g]))
    res = run_bass_kernel_spmd(nc, maps, core_ids=list(range(8)))
    out = np.empty((4, 4096, 1024), np.float32)
    for i, (b, g) in enumerate(cores):
        out[b, _tok_idx(g), :] = np.asarray(res.results[i]["out"], np.float32).reshape(1024, 2048).T
    return out
```

```python
import contextlib
import numpy as np
import concourse.bass as bass
import concourse.mybir as mybir
from concourse.bass_utils import run_bass_kernel_spmd

F32 = mybir.dt.float32
BF16 = mybir.dt.bfloat16
AF = mybir.ActivationFunctionType
ALU = mybir.AluOpType
EPS = 1e-6


class Tr:
    LIMIT = 6000

    def __init__(self, nc, stack):
        self.nc = nc
        self.stack = stack
        self.eng = {'pe': nc.tensor, 'act': nc.scalar, 'dve': nc.vector, 'pool': nc.gpsimd, 'sp': nc.sync}
        self.sems = []
        self.esem = {}
        self.known = {k: {} for k in self.eng}
        self.lastw = {}
        self.readers = {}
        self.dsem = {}
        self.dtot = {}
        self.nsem = 0

    def newsem(self):
        s = self.stack.enter_context(self.nc.semaphore(f"s{self.nsem}"))
        self.nsem += 1
        self.sems.append(s)
        return len(self.sems) - 1

    def _deps(self, reads, writes):
        deps = set()
        for k in reads:
            w = self.lastw.get(k)
            if w is not None:
                deps.add(w)
        for k in writes:
            w = self.lastw.get(k)
            if w is not None:
                deps.add(w)
            for r in self.readers.get(k, ()):
                deps.add(r)
        return deps

    def _wait(self, eng, deps):
        need = {}
        for (si, val, src) in deps:
            if src == eng and eng == 'pe':
                continue
            if src == 'dma':
                val = self.dtot[si][1]
            if self.known[eng].get(si, 0) >= val:
                continue
            need[si] = max(need.get(si, 0), val)
        for si, val in need.items():
            self.known[eng][si] = val
            self.eng[eng].wait_ge(self.sems[si], val)

    def _record(self, ev, reads, writes):
        for k in reads:
            self.readers.setdefault(k, []).append(ev)
        for k in writes:
            self.lastw[k] = ev
            self.readers[k] = []

    def op(self, eng, fn, reads=(), writes=()):
        self._wait(eng, self._deps(reads, writes))
        st = self.esem.get(eng)
        if st is None or st[1] >= self.LIMIT:
            st = self.esem[eng] = [self.newsem(), 0]
        st[1] += 1
        fn(self.eng[eng]).then_inc(self.sems[st[0]], 1)
        self._record((st[0], st[1], eng), reads, writes)

    def dma(self, q, out, in_, reads, writes, sname, **kw):
        ds = self.dsem.get(sname)
        if ds is None or ds[1] >= 16 * 1500:
            ds = self.dsem[sname] = [self.newsem(), 0]
            self.dtot[ds[0]] = ds
        self._wait(q, [d for d in self._deps(reads, writes) if d[0] != ds[0]])
        ds[1] += 16
        self.eng[q].dma_start(out=out, in_=in_, **kw).then_inc(self.sems[ds[0]], 16)
        self._record((ds[0], ds[1], 'dma'), reads, writes)

    def barrier(self):
        evs = set()
        for e, st in self.esem.items():
            if st[1] > 0:
                evs.add((st[0], st[1], e))
        for n, ds in self.dsem.items():
            evs.add((ds[0], ds[1], 'dma'))
        for k, w in self.lastw.items():
            evs.add(w)
        for k, rs in self.readers.items():
            for r in rs:
                evs.add(r)
        for e in self.eng:
            need = {}
            for (si, val, src) in evs:
                if src == e:
                    continue
                if src == 'dma':
                    val = self.dtot[si][1]
                if self.known[e].get(si, 0) >= val:
                    continue
                need[si] = max(need.get(si, 0), val)
            for si, val in need.items():
                self.known[e][si] = val
                self.eng[e].wait_ge(self.sems[si], val)
        self.lastw = {}
        self.readers = {}


def build():
    nc = bass.Bass("TRN2", target_bir_lowering=False)
    NL = 2

    def din(name, shape, dtype):
        return nc.dram_tensor(name, shape, dtype, kind="ExternalInput").ap()

    def dout(name, shape, dtype):
        return nc.dram_tensor(name, shape, dtype, kind="ExternalOutput").ap()

    def dint(name, shape, dtype):
        return nc.dram_tensor(name, shape, dtype, kind="Internal").ap()

    xT_d = din("xT", [8, 128, 4096], F32)
    cT_d = din("cT", [128, 8], F32)
    wada_d = din("w_ada", [NL, 1024, 6144], F32)
    bada_d = din("b_ada", [NL, 128, 48], F32)
    consts_d = din("consts", [128, 4, 128], F32)
    g1_d = din("g1", [NL, 128, 8], F32)
    win_d = din("w_in", [NL, 1024, 3072], F32)
    gqk_d = din("gqk", [NL, 128, 2], F32)
    g2_d = din("g2", [NL, 128, 8], F32)
    wo_d = din("w_o", [NL, 1024, 1024], F32)
    w1_d = din("w1", [NL, 1024, 4096], F32)
    w2_d = din("w2", [NL, 4096, 1024], F32)
    sbmask_d = din("sbmask", [128, 12, 512], F32)
    camask_d = din("camask", [2, 128, 1920], F32)
    strips_d = din("strips", [NL, 8, 128, 1920], F32)
    blend_d = din("blend", [128, 2], F32)
    out_d = dout("out", [8, 128, 2048], F32)
    qT_d = dint("qT", [8, 128, 2048], BF16)
    kT_d = [dint(f"kT{l}", [8, 128, 4096], BF16) for l in range(NL)]
    v_d = [dint(f"v{l}", [4096, 1024], BF16) for l in range(NL)]
    x1o_d = dint("x1own", [8, 128, 2048], F32)

    with contextlib.ExitStack() as gs:
        T = Tr(nc, gs)

        uid = [0]

        def sb(stack, name, shape, dtype):
            uid[0] += 1
            return stack.enter_context(nc.sbuf_tensor(f"{name}_u{uid[0]}", shape, dtype))

        psall = gs.enter_context(nc.psum_tensor("psall", [128, 8, 512], F32))
        ps = [psall[:, i, :] for i in range(8)]
        xT = sb(gs, "xT_sb", [128, 8, 2048], F32)
        actT = sb(gs, "actT", [128, 8, 2048], BF16)
        cb = sb(gs, "cb", [128, 4, 128], BF16)
        ones64 = sb(gs, "ones64", [128, 64], BF16)
        modvs = [sb(gs, f"modv{l}", [128, 48], F32) for l in range(NL)]
        gsc1s = [sb(gs, f"gsc1_{l}", [128, 8], F32) for l in range(NL)]
        gsc2s = [sb(gs, f"gsc2_{l}", [128, 8], F32) for l in range(NL)]
        gqks = [sb(gs, f"gqk{l}", [128, 2], F32) for l in range(NL)]
        cT = sb(gs, "cT_sb", [128, 8], F32)
        cact = sb(gs, "cact", [128, 8], F32)
        badas = [sb(gs, f"bada{l}", [128, 48], F32) for l in range(NL)]
        blend = sb(gs, "blend_sb", [128, 2], F32)
        sbmask = sb(gs, "sbmask_sb", [128, 12, 512], BF16)
        camask = sb(gs, "camask_sb", [128, 2, 1920], BF16)
        epsb = sb(gs, "epsb", [128, 1], F32)

        ones_f = cb[:, 0, :]
        blk64_f = cb[:, 1, :]
        negtri = cb[:, 2, :]
        negrest = cb[:, 3, :]

        T.dma('pool', cb[:], consts_d[:, :, :], [], ['cb'], 'c1')
        T.dma('sp', cT[:], cT_d[:, :], [], ['cT'], 'c2')
        T.dma('sp', blend[:], blend_d[:, :], [], ['blend'], 'c8')
        T.op('dve', lambda e: e.memset(ones64[:], 1.0), [], ['ones64'])
        T.op('dve', lambda e: e.memset(epsb[:], EPS), [], ['epsb'])
        for i in range(3):
            T.dma('pool', sbmask[:, 4 * i:4 * i + 4, :], sbmask_d[:, 4 * i:4 * i + 4, :], [], [('sbmask', i)], 'c3')
        for i in range(2):
            T.dma('pool', camask[:, i, :], camask_d[i], [], [('camask', i)], 'c4')
        T.op('act', lambda e: e.activation(out=cact[:], in_=cT[:], func=AF.Silu), ['cT'], ['cact'])
        g1s = [sb(gs, f"g1s{l}", [128, 8], F32) for l in range(NL)]
        g2s = [sb(gs, f"g2s{l}", [128, 8], F32) for l in range(NL)]
        for l in range(NL):
            T.dma('sp', badas[l][:], bada_d[l], [], [('bada', l)], 'c5')
            T.dma('sp', g1s[l][:], g1_d[l], [], [('g1', l)], 'c6')
            T.dma('sp', g2s[l][:], g2_d[l], [], [('g2', l)], 'c9')
            T.dma('sp', gqks[l][:], gqk_d[l], [], [('gqk', l)], 'c7')

        def load_x(G):
            for kc in range(8):
                T.dma('sp', xT[:, kc, :], xT_d[kc, :, 2048 * G:2048 * G + 2048], [], [('xT', j, kc) for j in range(4)], f'xl{kc % 2}')

        class ModStream:
            def __init__(self, l, chunks, st):
                self.l = l
                self.chunks = list(chunks)
                self.wa = [sb(st, f"wa{i}", [128, 8, 256], F32) for i in range(2)]
                self.src = wada_d[l].rearrange("(kc p) f -> p kc f", p=128)
                self.i = 0
                self._ld(0)

            def _ld(self, i):
                if i < len(self.chunks):
                    ch = self.chunks[i]
                    T.dma('sp', self.wa[i % 2][:], self.src[:, :, ch * 256:(ch + 1) * 256], [], [f'wa{i % 2}'], f'wa{i % 2}')

            @property
            def done(self):
                return self.i >= len(self.chunks)

            def step(self):
                if self.done:
                    return
                i = self.i
                self._ld(i + 1)
                ch = self.chunks[i]
                w = self.wa[i % 2]
                for fi in range(2):
                    fc = ch * 2 + fi
                    for kc in range(8):
                        T.op('pe', lambda e, fc=fc, fi=fi, kc=kc: e.matmul(
                            ps[7][:, fc:fc + 1], lhsT=w[:, kc, fi * 128:(fi + 1) * 128],
                            rhs=cact[:, kc:kc + 1], start=(kc == 0), stop=(kc == 7)),
                            [f'wa{i % 2}', 'cact'], ['ps7'])
                self.i += 1

            def drain(self):
                while not self.done:
                    self.step()

        def mod_fin(l, lo, hi):
            modv = modvs[l]
            T.op('dve', lambda e: e.tensor_tensor(out=modv[:, lo:hi], in0=ps[7][:, lo:hi], in1=badas[l][:, lo:hi], op=ALU.add),
                 ['ps7', ('bada', l)], [('modv', l, lo)])
            if lo <= 8 and hi >= 16:
                T.op('dve', lambda e: e.scalar_tensor_tensor(out=gsc1s[l][:], in0=modv[:, 8:16], scalar=1.0, in1=g1s[l][:],
                                                             op0=ALU.add, op1=ALU.mult), [('modv', l, lo), ('g1', l)], [('gsc1', l)])
            if lo <= 32 and hi >= 40:
                T.op('dve', lambda e: e.scalar_tensor_tensor(out=gsc2s[l][:], in0=modv[:, 32:40], scalar=1.0, in1=g2s[l][:],
                                                             op0=ALU.add, op1=ALU.mult), [('modv', l, lo), ('g2', l)], [('gsc2', l)])

        def emit_ln(st_tiles, j, gsc, modv, shoff):
            sqs, lnv, rstd, tmps = st_tiles
            tl = slice(j * 512, (j + 1) * 512)
            for kc in range(8):
                T.op('act', lambda e, kc=kc: e.activation(out=sqs[:, kc, :], in_=xT[:, kc, tl], func=AF.Square),
                     [('xT', j, kc)], [('sq', kc)])
            for kc in range(8):
                T.op('pe', lambda e, kc=kc: e.matmul(ps[4][:, :], lhsT=ones_f, rhs=sqs[:, kc, :], start=(kc == 0), stop=(kc == 7)),
                     [('sq', kc)], ['ps4'])
            T.op('act', lambda e: e.activation(out=lnv[:], in_=ps[4][:, :], func=AF.Ln, scale=1.0 / 1024, bias=epsb[:, 0:1]),
                 ['ps4', 'epsb'], ['lnv'])
            T.op('act', lambda e: e.activation(out=rstd[:], in_=lnv[:], func=AF.Exp, scale=-0.5), ['lnv'], ['rstd'])
            for kc in range(8):
                tm = tmps[kc % 2]
                T.op('dve', lambda e, kc=kc, tm=tm: e.scalar_tensor_tensor(
                    out=tm[:], in0=xT[:, kc, tl], scalar=gsc[:, kc:kc + 1], in1=rstd[:], op0=ALU.mult, op1=ALU.mult),
                    [('xT', j, kc), 'rstd', 'gsc'], [f'lt{kc % 2}'])
                T.op('act', lambda e, kc=kc, tm=tm: e.activation(
                    out=actT[:, kc, tl], in_=tm[:], func=AF.Identity, bias=modv[:, shoff + kc:shoff + kc + 1], scale=1.0),
                    [f'lt{kc % 2}', 'modv'], [('actT', j)])

        def ln_tiles(st):
            return (sb(st, "sq8", [128, 8, 512], BF16), sb(st, "lnv", [128, 512], F32),
                    sb(st, "rstd", [128, 512], F32), [sb(st, f"lt{i}", [128, 512], F32) for i in range(2)])

        blst = [None]

        def save_own(G, jj):
            stg = blst[0]
            ta, tb_ = 2 * jj, 2 * jj + 1
            for kc in range(8):
                T.op('dve', lambda e, kc=kc: e.tensor_scalar(out=stg[:, kc, :], in0=xT[:, kc, ta * 512:(ta + 1) * 512], scalar1=blend[:, 0:1],
                                                             scalar2=None, op0=ALU.mult), [('xT', ta, kc)], [('blst', kc)])
                T.op('dve', lambda e, kc=kc: e.scalar_tensor_tensor(out=stg[:, kc, :], in0=xT[:, kc, tb_ * 512:(tb_ + 1) * 512], scalar=blend[:, 1:2],
                                                                    in1=stg[:, kc, :], op0=ALU.mult, op1=ALU.add),
                     [('xT', tb_, kc), ('blst', kc)], [('blst', kc)])
            j = 2 * G + jj
            for kc in range(8):
                T.dma('sp', x1o_d[kc, :, j * 512:(j + 1) * 512], stg[:, kc, :], [('blst', kc)], [('x1own', j)], 'xs0')

        def load_own():
            for kc in range(8):
                T.dma('sp', xT[:, kc, :], x1o_d[kc], [('x1own', j) for j in range(4)], [('xT', j, kc) for j in range(4)], f'xl{kc % 2}')

        def phase_A(l, cgs, G, modspec=None, after_ln=None, mid=None, mid1=None):
            modv = modvs[l]
            gqk = gqks[l]
            with contextlib.ExitStack() as st:
                lt = ln_tiles(st)
                wq = [sb(st, f"wq{i}", [128, 8, 512], BF16) for i in range(2)]
                stage = [sb(st, f"stg{i}", [128, 2048], BF16) for i in range(4)]
                vst = [sb(st, f"vst{i}", [128, 512], BF16) for i in range(2)]
                sq2s = [sb(st, f"sq2_{i}", [128, 512], BF16) for i in range(2)]
                lnv2s = [sb(st, f"lnv2_{i}", [128, 512], F32) for i in range(2)]
                rstd2s = [sb(st, f"rstd2_{i}", [128, 512], F32) for i in range(2)]
                ncnt = [0]
                pending = []
                if after_ln is not None:
                    blst[0] = sb(st, "blst", [128, 8, 512], F32)
                wsrc = win_d[l].rearrange("(kc p) f -> p kc f", p=128)
                ms = ModStream(modspec[0], modspec[1], st) if modspec else None

                def ld(i):
                    cg = cgs[i]
                    T.dma('pool', wq[i % 2][:], wsrc[:, :, cg * 512:(cg + 1) * 512], [], [f'wq{i % 2}'], f'wq{i % 2}')
                ld(0)
                cnt = 0
                scnt = 0
                first_is_qk = cgs[0] in (0, 1, 3, 4)
                if not first_is_qk:
                    for j in range(4):
                        emit_ln(lt, j, gsc1s[l], modv, 0)
                    if after_ln:
                        after_ln()
                for ci, cg in enumerate(cgs):
                    if ci + 1 < len(cgs):
                        ld(ci + 1)
                    if ci == 1 and mid1:
                        mid1()
                    if ci == 2 and mid:
                        mid()
                    w = wq[ci % 2]
                    wk = f'wq{ci % 2}'
                    if cg in (0, 1, 3, 4):
                        is_ca = cg >= 3
                        is_q = cg in (0, 3)
                        tile_major = (ci == 0)
                        order = [(pp, j) for j in range(4) for pp in range(4)] if tile_major else [(pp, j) for pp in range(4) for j in range(4)]
                        for oi, (pp, j) in enumerate(order):
                            if tile_major and pp == 0:
                                if j == 0:
                                    emit_ln(lt, 0, gsc1s[l], modv, 0)
                                if j + 1 < 4:
                                    emit_ln(lt, j + 1, gsc1s[l], modv, 0)
                                    if j + 1 == 3 and after_ln:
                                        after_ln()
                            hp = pp + (4 if is_ca else 0)
                            if tile_major:
                                sg = stage[pp]
                                sk = f'stg{pp}'
                            else:
                                if j == 0:
                                    scnt += 1
                                sg = stage[scnt % 2]
                                sk = f'stg{scnt % 2}'
                            if True:
                                p_ = ps[cnt % 4]
                                pk = f'ps{cnt % 4}'
                                cnt += 1
                                tl = slice(j * 512, (j + 1) * 512)
                                for kc in range(8):
                                    T.op('pe', lambda e, kc=kc, p_=p_, w=w, pp=pp, tl=tl: e.matmul(
                                        p_[:, :], lhsT=w[:, kc, pp * 128:(pp + 1) * 128], rhs=actT[:, kc, tl],
                                        start=(kc == 0), stop=(kc == 7)), [wk, ('actT', j)], [pk])
                                if not is_ca:
                                    if cnt % 2 == 0:
                                        T.op('act', lambda e, p_=p_, sg=sg, tl=tl: e.activation(out=sg[:, tl], in_=p_[:, :], func=AF.Copy),
                                             [pk], [(sk, j)])
                                    else:
                                        T.op('dve', lambda e, p_=p_, sg=sg, tl=tl: e.tensor_copy(out=sg[:, tl], in_=p_[:, :]), [pk], [(sk, j)])
                                else:
                                    gi = 0 if is_q else 1
                                    ni = ncnt[0] % 2
                                    ncnt[0] += 1
                                    sq2, lnv2, rstd2, pn, pnk = sq2s[ni], lnv2s[ni], rstd2s[ni], ps[5 + ni], f'ps{5 + ni}'
                                    T.op('act', lambda e, p_=p_, sq2=sq2: e.activation(out=sq2[:], in_=p_[:, :], func=AF.Square), [pk], [f'sq2_{ni}'])

                                    def norm_tail(p_=p_, pk=pk, sg=sg, sk=sk, tl=tl, gi=gi, ni=ni, sq2=sq2, lnv2=lnv2, rstd2=rstd2, pn=pn, pnk=pnk, j=j):
                                        T.op('pe', lambda e: e.matmul(pn[:, :], lhsT=blk64_f, rhs=sq2[:], start=True, stop=True),
                                             [f'sq2_{ni}'], [pnk])
                                        T.op('act', lambda e: e.activation(out=lnv2[:], in_=pn[:, :], func=AF.Ln, scale=1.0 / 64,
                                                                           bias=epsb[:, 0:1]), [pnk, 'epsb'], [f'lnv2_{ni}'])
                                        T.op('act', lambda e: e.activation(out=rstd2[:], in_=lnv2[:], func=AF.Exp, scale=-0.5),
                                             [f'lnv2_{ni}'], [f'rstd2_{ni}'])
                                        T.op('dve', lambda e: e.scalar_tensor_tensor(
                                            out=sg[:, tl], in0=p_[:, :], scalar=gqk[:, gi:gi + 1], in1=rstd2[:], op0=ALU.mult, op1=ALU.mult),
                                            [pk, f'rstd2_{ni}', 'gqk'], [(sk, j)])
                                    prev = pending[:]
                                    del pending[:]
                                    for f_ in prev:
                                        f_()
                                    pending.append(norm_tail)
                            last_of_pp = (j == 3) if not tile_major else (oi >= 12)
                            if last_of_pp:
                                def store(sg=sg, sk=sk, hp=hp, is_q=is_q, sn=f'st{pp if tile_major else scnt % 2}'):
                                    dst = qT_d[hp] if is_q else kT_d[l][hp, :, 2048 * G:2048 * G + 2048]
                                    T.dma('sp', dst, sg[:], [(sk, jj) for jj in range(4)], [('qkd', is_q, hp)], sn)
                                    if ms:
                                        ms.step()
                                if is_ca:
                                    pending.append(store)
                                else:
                                    store()
                        for f_ in pending:
                            f_()
                        del pending[:]
                    else:
                        voff = 0 if cg == 2 else 512
                        for tb in range(16):
                            p_ = ps[cnt % 4]
                            pk = f'ps{cnt % 4}'
                            cnt += 1
                            for kc in range(8):
                                T.op('pe', lambda e, kc=kc, p_=p_, w=w, tb=tb: e.matmul(
                                    p_[:, :], lhsT=actT[:, kc, tb * 128:(tb + 1) * 128], rhs=w[:, kc, :],
                                    start=(kc == 0), stop=(kc == 7)), [wk, ('actT', tb // 4)], [pk])
                            vs = vst[tb % 2]
                            vk = f'vst{tb % 2}'
                            if tb % 2 == 0:
                                T.op('act', lambda e, p_=p_, vs=vs: e.activation(out=vs[:], in_=p_[:, :], func=AF.Copy), [pk], [vk])
                            else:
                                T.op('dve', lambda e, p_=p_, vs=vs: e.tensor_copy(out=vs[:], in_=p_[:, :]), [pk], [vk])
                            r0 = 2048 * G + tb * 128
                            T.dma('sp', v_d[l][r0:r0 + 128, voff:voff + 512], vs[:], [vk], [('v_d', tb, cg)], f'vs{tb % 2}')
                            if ms and tb % 4 == 3:
                                ms.step()
                if ms:
                    ms.drain()
                    mod_fin(modspec[0], modspec[2], modspec[3])
                T.barrier()

        def phase_attn(l, G):
            own = G is None
            nkt = 8 if own else 4 * (G + 1)
            EW = 1920 if own else 1408
            cmi = 1 if own else 0
            with contextlib.ExitStack() as st:
                kTs = [sb(st, f"kTs{i}", [128, 4096], BF16) for i in range(2)]
                vs_ = [sb(st, f"vsb{i}", [128, 32, 128], BF16) for i in range(2)]
                qs = [sb(st, f"qs{i}", [128, 2048], BF16) for i in range(2)]
                ebs = [sb(st, f"eb{i}", [128, 1920], F32) for i in range(2)]
                ebuf = [sb(st, f"e{i}", [128, 2, 512], F32) for i in range(3)]
                Lb = [sb(st, f"L{i}", [128, 2, 512], BF16) for i in range(2)]
                xc = [sb(st, f"xc{i}", [128, 2, 512], BF16) for i in range(2)]
                Ab = [sb(st, f"A{i}", [128, 2, 512], BF16) for i in range(2)]
                recs = [sb(st, f"rec{i}", [128, 512], F32) for i in range(2)]
                osb = [sb(st, f"osb{i}", [128, 512], F32) for i in range(2)]

                def load_hp(hp):
                    s = hp % 2
                    T.dma('sp', qs[s][:], qT_d[hp], [], [f'qs{s}'], f'lq{s}')
                    T.dma('sp', kTs[s][:, 0:nkt * 512], kT_d[l][hp, :, 0:nkt * 512], [], [f'kTs{s}'], f'lk{s}')
                    for k2 in range(0, nkt, 2):
                        srcv = v_d[l][k2 * 512:(k2 + 2) * 512, hp * 128:(hp + 1) * 128].rearrange("(n p) c -> p n c", p=128)
                        T.dma('sp', vs_[s][:, k2 * 4:(k2 + 2) * 4, :], srcv, [], [f'vsb{s}'], f'lv{s}')

                def prep_eb_dma(hc):
                    s = hc % 2
                    T.dma('sp', ebs[s][:, 0:EW], strips_d[l, hc, :, 0:EW], [], [f'eb{s}'], f'leb{s}')

                def prep_eb(hc):
                    s = hc % 2
                    T.op('act', lambda e: e.activation(out=ebs[s][:, 0:EW], in_=ebs[s][:, 0:EW], func=AF.Exp), [f'eb{s}'], [f'eb{s}'])
                    T.op('dve', lambda e: e.tensor_tensor(out=ebs[s][:, 0:EW], in0=ebs[s][:, 0:EW], in1=camask[:, cmi, 0:EW], op=ALU.mult),
                         [f'eb{s}'], [f'eb{s}'])

                def run_sb_all():
                    ul = []
                    hp_first = {}
                    for hp in range(4):
                      hp_first[len(ul)] = hp
                      for j in range(4):
                        if own:
                            nb = 4 * (2 * j + 2)
                        else:
                            nb = 4 * (4 * G + j + 1)
                        for kb in range(nb - 1, -1, -1):
                            kt, b = kb // 4, kb % 4
                            m = None
                            c0 = 0
                            if own:
                                if kt == 2 * j:
                                    m = 4 + b
                                elif kt == 2 * j + 1:
                                    m = 8 + b
                                    c0 = 128 * b
                            elif kt == 4 * G + j:
                                m = b
                                c0 = 128 * b
                            ul.append(dict(hp=hp, j=j, kb=kb, first=(kb == nb - 1), last=(kb == 0), m=m, c0=c0))
                    n = len(ul)

                    def bufs(u):
                        s = u['hp'] % 2
                        return kTs[s], vs_[s], qs[s], f'kTs{s}', f'vsb{s}', f'qs{s}'
                    hs = [slice(0, 64), slice(64, 128)]

                    def zi(c, i):
                        return 2 * (i % 2) + c

                    def QK(c, i):
                        u = ul[i]
                        K, V, Q, kk, vk, qk = bufs(u)
                        c0 = u['c0']
                        bi = zi(c, i)
                        T.op('pe', lambda e: e.matmul(ps[bi][:, c0:512], lhsT=K[hs[c], u['kb'] * 128:(u['kb'] + 1) * 128],
                                                      rhs=Q[hs[c], u['j'] * 512 + c0:(u['j'] + 1) * 512], start=True, stop=True),
                             [kk, qk], [f'ps{bi}'])

                    def EXP2(i):
                        c0 = ul[i]['c0']
                        p = 2 * (i % 2)
                        T.op('act', lambda e: e.activation(out=ebuf[i % 3][:, :, c0:512], in_=psall[:, p:p + 2, c0:512], func=AF.Exp, scale=0.125),
                             [f'ps{p}', f'ps{p + 1}'], [f'e{i % 3}'])

                    def MASK(c, i):
                        u = ul[i]
                        if u['m'] is None:
                            return
                        c0 = u['c0']
                        eb_ = ebuf[i % 3]
                        T.op('dve', lambda e: e.tensor_tensor(out=eb_[:, c, c0:512], in0=eb_[:, c, c0:512], in1=sbmask[:, u['m'], c0:512], op=ALU.mult),
                             [f'e{i % 3}'], [f'e{i % 3}'])

                    def MASK2(i):
                        u = ul[i]
                        if u['m'] is None:
                            return
                        c0 = u['c0']
                        eb_ = ebuf[i % 3]
                        mk = sbmask[:, u['m']:u['m'] + 1, c0:512].broadcast_to([128, 2, 512 - c0])
                        T.op('dve', lambda e: e.tensor_tensor(out=eb_[:, :, c0:512], in0=eb_[:, :, c0:512], in1=mk, op=ALU.mult),
                             [f'e{i % 3}'], [f'e{i % 3}'])

                    def LN2(i):
                        c0 = ul[i]['c0']
                        T.op('act', lambda e: e.activation(out=Lb[i % 2][:, :, c0:512], in_=ebuf[i % 3][:, :, c0:512], func=AF.Ln, bias=1.0, scale=1.0),
                             [f'e{i % 3}'], [f'L{i % 2}'])

                    def TRI(c, i):
                        u = ul[i]
                        c0 = u['c0']
                        T.op('pe', lambda e: e.matmul(ps[4 + c][:, c0:512], lhsT=negtri, rhs=Lb[i % 2][:, c, c0:512], start=u['first'], stop=True,
                                                      skip_group_check=True),
                             [f'L{i % 2}'], [f'ps{4 + c}'])

                    def EXPC2(i):
                        c0 = ul[i]['c0']
                        T.op('act', lambda e: e.activation(out=xc[i % 2][:, :, c0:512], in_=psall[:, 4:6, c0:512], func=AF.Exp),
                             ['ps4', 'ps5'], [f'xc{i % 2}'])

                    def REST(c, i):
                        u = ul[i]
                        if u['last']:
                            return
                        c0 = u['c0']
                        T.op('pe', lambda e: e.matmul(ps[4 + c][:, c0:512], lhsT=negrest, rhs=Lb[i % 2][:, c, c0:512], start=False, stop=True,
                                                      skip_group_check=True),
                             [f'L{i % 2}'], [f'ps{4 + c}'])

                    def MULT2(i):
                        c0 = ul[i]['c0']
                        T.op('dve', lambda e: e.tensor_tensor(out=Ab[i % 2][:, :, c0:512], in0=ebuf[i % 3][:, :, c0:512], in1=xc[i % 2][:, :, c0:512], op=ALU.mult),
                             [f'e{i % 3}', f'xc{i % 2}'], [f'A{i % 2}'])

                    def PV(c, i):
                        u = ul[i]
                        K, V, Q, kk, vk, qk = bufs(u)
                        hp = u['hp']
                        c0 = u['c0']
                        ob = ps[6 + u['j'] % 2]
                        ok = (f"ps{6 + u['j'] % 2}", c)
                        T.op('pe', lambda e: e.matmul(ob[hs[c], c0:512], lhsT=V[:, u['kb'], c * 64:(c + 1) * 64], rhs=Ab[i % 2][:, c, c0:512],
                                                      start=u['first'], stop=u['last'], skip_group_check=True), [vk, f'A{i % 2}'], [ok])
                        if u['last']:
                            tl = slice(u['j'] * 512, (u['j'] + 1) * 512)
                            if c == 0:
                                T.op('dve', lambda e: e.tensor_copy(out=actT[hs[c], hp, tl], in_=ob[hs[c], :]), [ok], [('actT', u['j'], hp, c)])
                            else:
                                T.op('act', lambda e: e.activation(out=actT[hs[c], hp, tl], in_=ob[hs[c], :], func=AF.Copy), [ok], [('actT', u['j'], hp, c)])

                    for c in range(2):
                        QK(c, 0)
                    for s_ in range(n + 2):
                        if (s_ - 3) in hp_first:
                            nh = hp_first[s_ - 3] + 1
                            load_hp(nh)
                            if nh == 4:
                                prep_eb_dma(0)
                        if (s_ - 9) in hp_first and hp_first[s_ - 9] == 3:
                            prep_eb(0)
                        if 0 <= s_ - 1 < n:
                            for c in range(2):
                                TRI(c, s_ - 1)
                        if s_ + 1 < n:
                            for c in range(2):
                                QK(c, s_ + 1)
                        if s_ < n:
                            EXP2(s_)
                            MASK2(s_)
                        if 0 <= s_ - 1 < n:
                            EXPC2(s_ - 1)
                        if s_ < n:
                            LN2(s_)
                        if 0 <= s_ - 2 < n:
                            for c in range(2):
                                PV(c, s_ - 2)
                        if 0 <= s_ - 1 < n:
                            for c in range(2):
                                REST(c, s_ - 1)
                            MULT2(s_ - 1)

                def run_ca_all():
                    def crange(r):
                        los, his = [], []
                        for sh in ((0, 8) if own else (0,)):
                            lo, hi = max(0, 2 * r - 8 - sh), min(7, 2 * r + 1 - sh)
                            if lo <= hi:
                                los.append(lo)
                                his.append(hi)
                        return 64 * min(los), 64 * (max(his) + 1)

                    chains = [[], []]
                    head_first = {}
                    for hp in range(4, 8):
                        for hh in range(2):
                            hc = (hp - 4) * 2 + hh
                            head_first[len(chains[0])] = (hc, hp, hh)
                            seg = [[], []]
                            for i2 in range(2):
                                for c in range(2):
                                    j = 2 * i2 + c
                                    if own:
                                        r0 = 4 if j == 0 else 0
                                        seg[c] += [dict(hp=hp, hh=hh, hc=hc, j=j, kb=8 * j - 4 + r, off=128 * (11 - r), first=(r == r0),
                                                        last=(r == 11), cr=crange(r)) for r in range(r0, 12)]
                                    else:
                                        Tg = 4 * G + j
                                        r0 = 4 if Tg == 0 else 0
                                        seg[c] += [dict(hp=hp, hh=hh, hc=hc, j=j, kb=4 * Tg - 4 + r, off=128 * (7 - r), first=(r == r0),
                                                        last=(r == 7), cr=crange(r)) for r in range(r0, 8)]
                            m_ = max(len(seg[0]), len(seg[1]))
                            for c in range(2):
                                chains[c] += seg[c] + [None] * (m_ - len(seg[c]))
                    n = len(chains[0])

                    def unit(c, i):
                        return chains[c][i] if 0 <= i < n else None

                    def zb(c, i):
                        return ps[c * 2 + i % 2], f'ps{c * 2 + i % 2}'

                    def QK(c, i):
                        u = unit(c, i)
                        if u is None:
                            return
                        s = u['hp'] % 2
                        hsl = slice(u['hh'] * 64, (u['hh'] + 1) * 64)
                        p_, pk = zb(c, i)
                        a0, a1 = u['cr']
                        T.op('pe', lambda e: e.matmul(p_[:, a0:a1], lhsT=kTs[s][hsl, u['kb'] * 128:(u['kb'] + 1) * 128],
                                                      rhs=qs[s][hsl, u['j'] * 512 + a0:u['j'] * 512 + a1], start=True, stop=True),
                             [f'kTs{s}', f'qs{s}'], [pk])

                    def EXP(c, i):
                        u = unit(c, i)
                        if u is None:
                            return
                        p_, pk = zb(c, i)
                        a0, a1 = u['cr']
                        T.op('act', lambda e: e.activation(out=ebuf[i % 3][:, c, a0:a1], in_=p_[:, a0:a1], func=AF.Exp, scale=0.125), [pk], [(f'e{i % 3}', c)])

                    def MULT(c, i):
                        u = unit(c, i)
                        if u is None:
                            return
                        off = u['off']
                        a0, a1 = u['cr']
                        EB = ebs[u['hc'] % 2]
                        T.op('dve', lambda e: e.tensor_tensor(out=Ab[i % 2][:, c, a0:a1], in0=ebuf[i % 3][:, c, a0:a1], in1=EB[:, off + a0:off + a1], op=ALU.mult),
                             [(f'e{i % 3}', c), f"eb{u['hc'] % 2}"], [(f'A{i % 2}', c)])

                    def PV(c, i):
                        u = unit(c, i)
                        if u is None:
                            return
                        hp, hh = u['hp'], u['hh']
                        s = hp % 2
                        V = vs_[s]
                        vk = f'vsb{s}'
                        hsl = slice(hh * 64, (hh + 1) * 64)
                        ob = ps[6 + c]
                        db = ps[4 + c]
                        ok = (f'ps{6 + c}', hh)
                        dk = (f'ps{4 + c}', hh)
                        a0, a1 = u['cr']
                        T.op('pe', lambda e: e.matmul(ob[hsl, a0:a1], lhsT=V[:, u['kb'], hh * 64:(hh + 1) * 64], rhs=Ab[i % 2][:, c, a0:a1],
                                                      start=u['first'], stop=u['last'], skip_group_check=True), [vk, (f'A{i % 2}', c)], [ok])
                        T.op('pe', lambda e: e.matmul(db[hsl, a0:a1], lhsT=ones64[:, :], rhs=Ab[i % 2][:, c, a0:a1],
                                                      start=u['first'], stop=u['last'], skip_group_check=True), [(f'A{i % 2}', c)], [dk])
                        if u['last']:
                            tl = slice(u['j'] * 512, (u['j'] + 1) * 512)
                            rc = recs[c]
                            oc = osb[c]
                            T.op('act', lambda e: e.activation(out=rc[hsl, :], in_=db[hsl, :], func=AF.Ln), [dk], [f'rec{c}'])
                            T.op('dve', lambda e: e.tensor_copy(out=oc[hsl, :], in_=ob[hsl, :]), [ok], [f'osb{c}'])
                            T.op('act', lambda e: e.activation(out=rc[hsl, :], in_=rc[hsl, :], func=AF.Exp, scale=-1.0), [f'rec{c}'], [f'rec{c}'])
                            T.op('dve', lambda e: e.tensor_tensor(out=actT[hsl, hp, tl], in0=oc[hsl, :], in1=rc[hsl, :], op=ALU.mult),
                                 [f'osb{c}', f'rec{c}'], [('actT', u['j'], hp, hh)])

                    for c in range(2):
                        QK(c, 0)
                    for s_ in range(n + 1):
                        if (s_ - 2) in head_first:
                            hc, hp, hh = head_first[s_ - 2]
                            if hc + 1 < 8:
                                prep_eb_dma(hc + 1)
                            if hh == 0 and hp + 1 < 8:
                                load_hp(hp + 1)
                        if (s_ - 8) in head_first:
                            hc, hp, hh = head_first[s_ - 8]
                            if hc + 1 < 8:
                                prep_eb(hc + 1)
                        for c in range(2):
                            QK(c, s_ + 1)
                        for c in range(2):
                            EXP(c, s_)
                        for c in range(2):
                            MULT(c, s_)
                        for c in range(2):
                            PV(c, s_ - 1)

                load_hp(0)
                run_sb_all()
                T.barrier()
                run_ca_all()
                T.barrier()

        def phase_wo(l):
            modv = modvs[l]
            with contextlib.ExitStack() as st:
                wo = sb(st, "wo_sb", [128, 8, 1024], BF16)
                wsrc = wo_d[l].rearrange("(kc p) f -> p kc f", p=128)
                for h in range(2):
                    T.dma('pool', wo[:, :, h * 512:(h + 1) * 512], wsrc[:, :, h * 512:(h + 1) * 512], [], [('wo', h)], f'wo{h}')
                cnt = 0
                for j in range(4):
                    tl = slice(j * 512, (j + 1) * 512)
                    for d in range(8):
                        p_ = ps[cnt % 4]
                        pk = f'ps{cnt % 4}'
                        cnt += 1
                        for hp in range(8):
                            T.op('pe', lambda e, hp=hp, p_=p_, d=d, tl=tl: e.matmul(
                                p_[:, :], lhsT=wo[:, hp, d * 128:(d + 1) * 128], rhs=actT[:, hp, tl], start=(hp == 0), stop=(hp == 7)),
                                [('wo', d // 4)], [pk])
                        T.op('dve', lambda e, p_=p_, d=d, tl=tl: e.scalar_tensor_tensor(
                            out=xT[:, d, tl], in0=p_[:, :], scalar=modv[:, 16 + d:17 + d], in1=xT[:, d, tl], op0=ALU.mult, op1=ALU.add),
                            [pk, ('xT', j, d)], [('xT', j, d)])
                T.barrier()

        def phase_mlp(l, modspec=None, final=False):
            modv = modvs[l]
            with contextlib.ExitStack() as st:
                ms = ModStream(modspec[0], modspec[1], st) if modspec else None
                lt = ln_tiles(st)
                w1s = [sb(st, f"w1s{i}", [128, 8, 512], BF16) for i in range(2)]
                w2s = [sb(st, f"w2s{i}", [128, 4, 1024], BF16) for i in range(2)]
                h1 = [sb(st, f"h1_{i}", [128, 4, 512], BF16) for i in range(2)]
                rt = [sb(st, f"rt{i}", [128, 512], F32) for i in range(2)]
                w1src = w1_d[l].rearrange("(kc p) f -> p kc f", p=128)

                def ld(fg):
                    s = fg % 2
                    T.dma('pool', w1s[s][:], w1src[:, :, fg * 512:(fg + 1) * 512], [], [f'w1s{s}'], f'w1s{s}')
                    T.dma('pool', w2s[s][:], w2_d[l, fg * 512:(fg + 1) * 512, :].rearrange("(fi p) d -> p fi d", p=128), [], [f'w2s{s}'], f'w2s{s}')
                ld(0)
                cnts = dict(c1=0, c2=0)

                def stage1(fg, j, hb, hk):
                    s = fg % 2
                    tl = slice(j * 512, (j + 1) * 512)
                    for fi in range(4):
                        c1 = cnts['c1']
                        cnts['c1'] += 1
                        p_ = ps[c1 % 4]
                        pk = f'ps{c1 % 4}'
                        r_ = rt[c1 % 2]
                        rk = f'rt{c1 % 2}'
                        for kc in range(8):
                            T.op('pe', lambda e, kc=kc: e.matmul(
                                p_[:, :], lhsT=w1s[s][:, kc, fi * 128:(fi + 1) * 128], rhs=actT[:, kc, tl], start=(kc == 0), stop=(kc == 7)),
                                [f'w1s{s}', ('actT', j)], [pk])
                        T.op('act', lambda e: e.activation(out=r_[:], in_=p_[:, :], func=AF.Relu), [pk], [rk])
                        T.op('dve', lambda e: e.tensor_tensor(out=hb[:, fi, :], in0=r_[:], in1=r_[:], op=ALU.mult), [rk], [(hk, fi)])

                def stage2(fg, j, hb, hk):
                    s = fg % 2
                    tl = slice(j * 512, (j + 1) * 512)
                    for d in range(8):
                        c2 = cnts['c2']
                        cnts['c2'] += 1
                        p_ = ps[4 + c2 % 3]
                        pk = f'ps{4 + c2 % 3}'
                        for fi in range(4):
                            T.op('pe', lambda e, fi=fi: e.matmul(
                                p_[:, :], lhsT=w2s[s][:, fi, d * 128:(d + 1) * 128], rhs=hb[:, fi, :], start=(fi == 0), stop=(fi == 3)),
                                [f'w2s{s}', (hk, fi)], [pk])
                        T.op('dve', lambda e: e.scalar_tensor_tensor(
                            out=xT[:, d, tl], in0=p_[:, :], scalar=modv[:, 40 + d:41 + d], in1=xT[:, d, tl], op0=ALU.mult, op1=ALU.add),
                            [pk, ('xT', j, d)], [('xT', j, d)])
                    if final and fg == 7:
                        for kc in range(8):
                            T.dma('sp', out_d[kc, :, tl], xT[:, kc, tl], [('xT', j, kc)], [('out', kc, j)], f'xl{kc % 2}')
                    if ms:
                        ms.step()

                its = [(fg, j) for fg in range(8) for j in range(4)]
                prev = None
                for k, (fg, j) in enumerate(its):
                    if fg == 0:
                        if j == 0:
                            emit_ln(lt, 0, gsc2s[l], modv, 24)
                        if j + 1 < 4:
                            emit_ln(lt, j + 1, gsc2s[l], modv, 24)
                    hb = h1[k % 2]
                    hk = f'h1_{k % 2}'
                    stage1(fg, j, hb, hk)
                    if prev is not None:
                        stage2(*prev)
                    if j == 0 and fg + 1 < 8:
                        ld(fg + 1)
                    prev = (fg, j, hb, hk)
                stage2(*prev)
                if ms:
                    ms.drain()
                    mod_fin(modspec[0], modspec[2], modspec[3])
                T.barrier()

        load_x(0)
        T.barrier()
        with contextlib.ExitStack() as st0:
            ms0 = ModStream(0, range(0, 8), st0)
            ms0.drain()
            mod_fin(0, 0, 16)
            T.barrier()
        for G in range(2):
            phase_A(0, [0, 1, 2, 3, 4, 5], G, modspec=(0, range(8, 24), 16, 48) if G == 0 else None)
            phase_attn(0, G)
            phase_wo(0)
            phase_mlp(0, modspec=(1, range(0, 24), 0, 48) if G == 0 else None)
            phase_A(1, [1, 2, 4, 5], G, after_ln=(lambda G=G: save_own(G, 0)), mid1=(lambda G=G: save_own(G, 1)),
                    mid=(lambda: load_x(1)) if G == 0 else load_own)
        phase_A(1, [0, 3], None)
        phase_attn(1, None)
        phase_wo(1)
        phase_mlp(1, final=True)
        T.barrier()
    return nc


def _consts():
    jj = np.arange(128)[:, None]
    ss = np.arange(128)[None, :]
    ones = np.ones((128, 128), np.float32)
    blk = ((jj // 64) == (ss // 64)).astype(np.float32)
    negtri = -(jj >= ss).astype(np.float32)
    negrest = -(jj < ss).astype(np.float32)
    return np.ascontiguousarray(np.stack([ones, blk, negtri, negrest], axis=1))


def _sbmask(g):
    s = np.arange(128)[:, None, None]
    b = np.arange(4)[None, :, None]
    t = np.arange(512)[None, None, :]
    diag = ((128 * b + s) < t).astype(np.float32)
    onesm = np.ones_like(diag)
    zer = np.zeros_like(diag)
    mA, mB = (diag, zer) if g == 0 else (onesm, diag)
    return np.ascontiguousarray(np.concatenate([diag, mA, mB], axis=1))


def _camask(g):
    s = np.arange(128)[:, None]
    c = np.arange(1920)[None, :]
    d0 = c // 64 - s // 64 - 6
    d1 = c // 64 - s // 64 + 8 * g - 14
    m0 = ((d0 >= 0) & (d0 <= 8)).astype(np.float32)
    m1 = ((d1 >= 0) & (d1 <= 8)).astype(np.float32)
    return np.ascontiguousarray(np.stack([m0, m1], axis=0))


def _strips(rel_bias, g):
    s = np.arange(128)[:, None]
    c = np.arange(1920)[None, :]
    idx0 = np.clip(c - s - 384, -128, 128) + 128
    idx1 = np.clip(c - s + 512 * g - 896, -128, 128) + 128
    return np.ascontiguousarray(np.stack([rel_bias[0][:, idx0], rel_bias[1][:, idx1]], axis=0))


def _tok_idx(g):
    return np.concatenate([np.arange(512 * (2 * j + g), 512 * (2 * j + g) + 512) for j in range(4)])


_NC = {}


def kernel(x, c, g_norm1, w_in, g_q, g_k, rel_bias, w_o, g_norm2, w1, w2, w_ada, b_ada):
    x = np.asarray(x, np.float32)
    c = np.asarray(c, np.float32)
    f = lambda a: np.ascontiguousarray(np.asarray(a, np.float32))
    g_norm1, w_in, g_q, g_k, rel_bias, w_o, g_norm2, w1, w2, w_ada, b_ada = map(
        f, (g_norm1, w_in, g_q, g_k, rel_bias, w_o, g_norm2, w1, w2, w_ada, b_ada))
    L = 2
    cores = [(b, g) for b in range(4) for g in range(2)]
    consts = _consts()
    g1l = np.ascontiguousarray(g_norm1.reshape(L, 8, 128).transpose(0, 2, 1))
    g2l = np.ascontiguousarray(g_norm2.reshape(L, 8, 128).transpose(0, 2, 1))
    badal = np.ascontiguousarray(b_ada.reshape(L, 48, 128).transpose(0, 2, 1))
    gqk = np.ascontiguousarray(np.stack([np.tile(g_q, (1, 2)), np.tile(g_k, (1, 2))], axis=2))
    strips = [_strips(rel_bias, g) for g in range(2)]
    sbm = [_sbmask(g) for g in range(2)]
    cam = [_camask(g) for g in range(2)]
    blends = [np.ascontiguousarray(np.tile(np.array([[1.0, 0.0]] if g == 0 else [[0.0, 1.0]], np.float32), (128, 1))) for g in range(2)]
    xTb = [np.ascontiguousarray(x[b].T).reshape(8, 128, 4096) for b in range(4)]
    if 'F' not in _NC:
        _NC['F'] = build()
    nc = _NC['F']
    maps = []
    for (b, g) in cores:
        maps.append(dict(xT=xTb[b], cT=np.ascontiguousarray(c[b].reshape(8, 128).T), w_ada=w_ada, b_ada=badal, consts=consts,
                         g1=g1l, w_in=w_in, gqk=gqk, g2=g2l, w_o=w_o, w1=w1, w2=w2, sbmask=sbm[g], camask=cam[g],
                         strips=strips[g], blend=blends[g]))
    res = run_bass_kernel_spmd(nc, maps, core_ids=list(range(8)))
    out = np.empty((4, 4096, 1024), np.float32)
    for i, (b, g) in enumerate(cores):
        out[b, _tok_idx(g), :] = np.asarray(res.results[i]["out"], np.float32).reshape(1024, 2048).T
    return out
```
